# Optimizing a Trainium2 kernel written in Bass

```python
import math
import jax, jax.numpy as jnp
from jax import lax
import numpy as np

D_MODEL = 1024
BATCH = 8
SEQ = 4096
DEPTH = 2

N_A_LAYERS = DEPTH // 2
N_B_LAYERS = DEPTH - N_A_LAYERS
GLA_HEADS = 4
GLA_DK = D_MODEL // 2
GLA_DV = D_MODEL
GLA_HK = GLA_DK // GLA_HEADS
GLA_HV = GLA_DV // GLA_HEADS
GLA_GATE_RANK = 16
GLA_GATE_NORM = 16.0
GLA_CHUNK = 64
DIFF_HEADS = 8
DIFF_HD = D_MODEL // (2 * DIFF_HEADS)
DIFF_VD = 2 * DIFF_HD
DIFF_QK = DIFF_HEADS * 2 * DIFF_HD
DIFF_V = DIFF_HEADS * DIFF_VD
Q_BLOCK = 128
ROPE_THETA = 10000.0
D_FF = 2816
CONV_WIDTH = 3
EPS = 1e-6

kernel_name = 'yoco_gla_diffattn_convffn'


def rmsnorm(x, w):
    xf = x.astype(jnp.float32)
    y = xf * lax.rsqrt(jnp.mean(xf * xf, axis=-1, keepdims=True) + EPS)
    return (y * w.astype(jnp.float32)).astype(x.dtype)


def rope_tables(seq):
    pos = jnp.arange(seq, dtype=jnp.float32)
    inv = ROPE_THETA ** (-jnp.arange(0, DIFF_HD, 2, dtype=jnp.float32) / DIFF_HD)
    f = pos[:, None] * inv[None, :]
    emb = jnp.concatenate([f, f], axis=-1)
    return jnp.cos(emb), jnp.sin(emb)


def apply_rope(x, cos, sin):
    c = cos[None, :, None, None, :]
    s = sin[None, :, None, None, :]
    x1, x2 = jnp.split(x, 2, axis=-1)
    rot = jnp.concatenate([-x2, x1], axis=-1)
    return (x.astype(jnp.float32) * c + rot.astype(jnp.float32) * s).astype(x.dtype)


def gla_mix(h, w_qkvg, w_gk1, w_gk2, b_gk, onorm_w, w_o):
    B, S, _ = h.shape
    C = GLA_CHUNK
    N = S // C
    proj = h @ w_qkvg
    q, k, v, g = jnp.split(proj, [GLA_DK, 2 * GLA_DK, 2 * GLA_DK + GLA_DV], axis=-1)
    gk = jax.nn.log_sigmoid(((h @ w_gk1) @ w_gk2 + b_gk).astype(jnp.float32)) / GLA_GATE_NORM

    def heads_k(t):
        return t.astype(jnp.float32).reshape(B, N, C, GLA_HEADS, GLA_HK).transpose(0, 3, 1, 2, 4)

    q = heads_k(q) * (GLA_HK ** -0.5)
    k = heads_k(k)
    gk = heads_k(gk)
    v = v.astype(jnp.float32).reshape(B, N, C, GLA_HEADS, GLA_HV).transpose(0, 3, 1, 2, 4)

    b = jnp.cumsum(gk, axis=3)
    b_last = b[:, :, :, -1:, :]
    q_in = q * jnp.exp(b)
    k_in = k * jnp.exp(-b)
    k_end = k * jnp.exp(b_last - b)

    causal = jnp.tril(jnp.ones((C, C), dtype=bool))
    scores = jnp.einsum('bhncd,bhnjd->bhncj', q_in, k_in)
    scores = jnp.where(causal, scores, 0.0)
    o_intra = jnp.einsum('bhncj,bhnje->bhnce', scores, v)

    contrib = jnp.einsum('bhncd,bhnce->bhnde', k_end, v)
    decay = jnp.exp(b_last[:, :, :, 0, :])

    def step(state, inp):
        d, c = inp
        return d[..., None] * state + c, state

    init = jnp.zeros((B, GLA_HEADS, GLA_HK, GLA_HV), jnp.float32)
    _, s_prev = lax.scan(step, init, (jnp.moveaxis(decay, 2, 0), jnp.moveaxis(contrib, 2, 0)))
    s_prev = jnp.moveaxis(s_prev, 0, 2)
    o_inter = jnp.einsum('bhncd,bhnde->bhnce', q_in, s_prev)

    o = (o_intra + o_inter).transpose(0, 2, 3, 1, 4).reshape(B, S, GLA_HEADS, GLA_HV)
    o = rmsnorm(o.astype(h.dtype), onorm_w).reshape(B, S, GLA_DV)
    o = o * jax.nn.silu(g)
    return o @ w_o


def shared_kv(h, kv_norm_w, w_kv, cos, sin):
    B, S, _ = h.shape
    kv = rmsnorm(h, kv_norm_w) @ w_kv
    k, v = jnp.split(kv, [DIFF_QK], axis=-1)
    k = apply_rope(k.reshape(B, S, DIFF_HEADS, 2, DIFF_HD), cos, sin)
    v = v.reshape(B, S, DIFF_HEADS, DIFF_VD)
    return k, v


def diff_attn(h, k_sh, v_sh, w_q, lam_p, subln_w, w_o, lam_init, cos, sin):
    B, S, _ = h.shape
    q = apply_rope((h @ w_q).reshape(B, S, DIFF_HEADS, 2, DIFF_HD), cos, sin)
    lp = lam_p.astype(jnp.float32)
    lam = jnp.exp(jnp.sum(lp[0] * lp[1])) - jnp.exp(jnp.sum(lp[2] * lp[3])) + lam_init
    nqb = S // Q_BLOCK
    qb = jnp.moveaxis(q.reshape(B, nqb, Q_BLOCK, DIFF_HEADS, 2, DIFF_HD), 1, 0)
    kpos = jnp.arange(S)
    scale = DIFF_HD ** -0.5

    def block(args):
        qblk, i = args
        s = jnp.einsum('bqhcd,bkhcd->bhcqk', qblk, k_sh).astype(jnp.float32) * scale
        qpos = i * Q_BLOCK + jnp.arange(Q_BLOCK)
        mask = kpos[None, :] <= qpos[:, None]
        s = jnp.where(mask[None, None, None], s, -jnp.inf)
        p = jax.nn.softmax(s, axis=-1)
        a = p[:, :, 0] - lam * p[:, :, 1]
        return jnp.einsum('bhqk,bkhe->bqhe', a.astype(v_sh.dtype), v_sh)

    o = lax.map(block, (qb, jnp.arange(nqb)))
    o = jnp.moveaxis(o, 0, 1).reshape(B, S, DIFF_HEADS, DIFF_VD)
    o = rmsnorm(o, subln_w) * (1.0 - lam_init)
    return o.reshape(B, S, DIFF_V) @ w_o


def conv_ffn(h, w_in, conv_w, conv_b, w_out):
    S = h.shape[1]
    u = h @ w_in
    up = jnp.pad(u, ((0, 0), (CONV_WIDTH - 1, 0), (0, 0)))
    c = sum(up[:, j:j + S, :] * conv_w[j] for j in range(CONV_WIDTH)) + conv_b
    a, g = jnp.split(c, 2, axis=-1)
    return (jax.nn.silu(g) * a) @ w_out


def setup_inputs(seed: int = 0) -> dict:
    key = jax.random.key(seed)
    ks = jax.random.split(key, 24)

    def nrm(k, shape, scale):
        return jax.random.normal(k, shape, jnp.float32) * scale

    def gain(k, shape):
        return 1.0 + 0.02 * jax.random.normal(k, shape, jnp.float32)

    D = D_MODEL
    return {
        'x': nrm(ks[0], (BATCH, SEQ, D), 1.0),
        'attn_norm_w': gain(ks[1], (DEPTH, D)),
        'ffn_norm_w': gain(ks[2], (DEPTH, D)),
        'gla_w_qkvg': nrm(ks[3], (N_A_LAYERS, D, 2 * GLA_DK + 2 * GLA_DV), D ** -0.5),
        'gla_w_gk1': nrm(ks[4], (N_A_LAYERS, D, GLA_GATE_RANK), D ** -0.5),
        'gla_w_gk2': nrm(ks[5], (N_A_LAYERS, GLA_GATE_RANK, GLA_DK), GLA_GATE_RANK ** -0.5),
        'gla_b_gk': nrm(ks[6], (N_A_LAYERS, GLA_DK), 0.1),
        'gla_onorm_w': gain(ks[7], (N_A_LAYERS, GLA_HV)),
        'gla_w_o': nrm(ks[8], (N_A_LAYERS, GLA_DV, D), GLA_DV ** -0.5),
        'kv_norm_w': gain(ks[9], (D,)),
        'w_kv': nrm(ks[10], (D, DIFF_QK + DIFF_V), D ** -0.5),
        'diff_w_q': nrm(ks[11], (N_B_LAYERS, D, DIFF_QK), D ** -0.5),
        'diff_lambda': nrm(ks[12], (N_B_LAYERS, 4, DIFF_HD), 0.1),
        'diff_subln_w': gain(ks[13], (N_B_LAYERS, DIFF_VD)),
        'diff_w_o': nrm(ks[14], (N_B_LAYERS, DIFF_V, D), DIFF_V ** -0.5),
        'ffn_w_in': nrm(ks[15], (DEPTH, D, 2 * D_FF), D ** -0.5),
        'ffn_conv_w': nrm(ks[16], (DEPTH, CONV_WIDTH, 2 * D_FF), CONV_WIDTH ** -0.5),
        'ffn_conv_b': nrm(ks[17], (DEPTH, 2 * D_FF), 0.02),
        'ffn_w_out': nrm(ks[18], (DEPTH, D_FF, D), D_FF ** -0.5),
        'final_norm_w': gain(ks[19], (D,)),
    }


def reference(x, attn_norm_w, ffn_norm_w, gla_w_qkvg, gla_w_gk1, gla_w_gk2, gla_b_gk, gla_onorm_w, gla_w_o,
              kv_norm_w, w_kv, diff_w_q, diff_lambda, diff_subln_w, diff_w_o,
              ffn_w_in, ffn_conv_w, ffn_conv_b, ffn_w_out, final_norm_w):
    S = x.shape[1]
    cos, sin = rope_tables(S)
    h = x
    k_sh = None
    v_sh = None
    for l in range(DEPTH):
        if l == N_A_LAYERS:
            k_sh, v_sh = shared_kv(h, kv_norm_w, w_kv, cos, sin)
        a_in = rmsnorm(h, attn_norm_w[l])
        if l < N_A_LAYERS:
            h = h + gla_mix(a_in, gla_w_qkvg[l], gla_w_gk1[l], gla_w_gk2[l], gla_b_gk[l], gla_onorm_w[l], gla_w_o[l])
        else:
            j = l - N_A_LAYERS
            lam_init = 0.8 - 0.6 * math.exp(-0.3 * l)
            h = h + diff_attn(a_in, k_sh, v_sh, diff_w_q[j], diff_lambda[j], diff_subln_w[j], diff_w_o[j], lam_init, cos, sin)
        h = h + conv_ffn(rmsnorm(h, ffn_norm_w[l]), ffn_w_in[l], ffn_conv_w[l], ffn_conv_b[l], ffn_w_out[l])
    return rmsnorm(h, final_norm_w)
```

```python
import numpy as np
from contextlib import ExitStack
import concourse.bass as bass
import concourse.mybir as mybir
from concourse.bass_utils import run_bass_kernel_spmd

F32 = mybir.dt.float32
BF16 = mybir.dt.bfloat16
AF = mybir.ActivationFunctionType
ALU = mybir.AluOpType
AX = mybir.AxisListType

D = 1024
NKC = 8
DFF = 2816
NFC = 22
EPS = 1e-6
LAM_INIT = 0.8 - 0.6 * float(np.exp(-0.3 * 1))


class Buf:
    __slots__ = ("name", "w", "r", "dsem", "dcnt", "excl")

    def __init__(self, name, excl=False):
        self.name = name
        self.excl = excl
        self.w = None
        self.r = {}
        self.dsem = None
        self.dcnt = 0


class Eng:
    def __init__(self, name, e, sem):
        self.name, self.e, self.sem = name, e, sem
        self.cnt = 0
        self.seen = {}


class Ctx:
    def __init__(self, nc, stack):
        self.nc = nc
        self.stack = stack
        self.engs = {}
        for n, a in [("pe", "tensor"), ("act", "scalar"), ("dve", "vector"), ("pool", "gpsimd"), ("sp", "sync")]:
            self.engs[n] = Eng(n, getattr(nc, a), stack.enter_context(nc.semaphore("s_" + n)))
        self.owners = []
        self.nwait = 0
        self.nins = 0
        self.uid = 0
        self.limit = None
        self.last_desc = None

    def buf(self, name, excl=False):
        self.uid += 1
        return Buf("%s_%d" % (name, self.uid), excl)

    def _wait(self, E, toks, rawkeys):
        best = {}
        for t in toks:
            if t is None:
                continue
            k, sem, val, en = t
            if en == E.name and en == "pe":
                continue
            if k not in best or best[k][2] < val:
                best[k] = t
        for k, (_, sem, val, en) in best.items():
            if E.seen.get(k, 0) < val:
                E.e.wait_ge(sem, val)
                E.seen[k] = val
                self.nwait += 1

    def _deps(self, E, reads, writes):
        toks = []
        rawkeys = set()
        for b in reads:
            if b.w is not None:
                toks.append(b.w)
                if b.w[3] == E.name:
                    rawkeys.add(b.w[0])
            if b.excl:
                toks.extend(t for t in b.r.values() if t[3] != E.name)
        for b in writes:
            if b.w is not None:
                toks.append(b.w)
            toks.extend(b.r.values())
        self._wait(E, toks, rawkeys)

    def _commit(self, tok, reads, writes):
        k = tok[0]
        for b in reads:
            if k not in b.r or b.r[k][2] < tok[2]:
                b.r[k] = tok
        for b in writes:
            b.w = tok
            b.r = {}

    def op(self, en, emit, reads=(), writes=()):
        E = self.engs[en]
        if self.limit is not None and self.nins >= self.limit:
            return None
        self._deps(E, reads, writes)
        ins = emit(E.e)
        self.last_desc = ins
        E.cnt += 1
        ins.then_inc(E.sem, 1)
        self.nins += 1
        tok = ("e_" + en, E.sem, E.cnt, en)
        self._commit(tok, reads, writes)
        return tok

    def dma(self, q, out, in_, owner, reads=(), writes=(), **kw):
        E = self.engs[q]
        if self.limit is not None and self.nins >= self.limit:
            return None
        self._deps(E, reads, writes)
        if owner.dsem is None:
            owner.dsem = self.stack.enter_context(self.nc.semaphore("d_" + owner.name))
            self.owners.append(owner)
        ins = E.e.dma_start(out=out, in_=in_, **kw)
        owner.dcnt += 16
        ins.then_inc(owner.dsem, 16)
        self.nins += 1
        tok = ("d_" + owner.name, owner.dsem, owner.dcnt, "dma")
        self._commit(tok, reads, writes)
        return tok

    def barrier(self):
        toks = [("e_" + E.name, E.sem, E.cnt, E.name) for E in self.engs.values() if E.cnt > 0]
        toks += [("d_" + b.name, b.dsem, b.dcnt, "dma") for b in self.owners]
        for E in self.engs.values():
            mine = [t for t in toks if t[3] != E.name]
            self._wait(E, mine, set())


class Rot:
    def __init__(self, items):
        self.items = items
        self.i = 0

    def next(self):
        it = self.items[self.i % len(self.items)]
        self.i += 1
        return it


def build(S=4096, phases="ABQKDE", debug=False, limit=None):
    NT = S // 128
    NB = S // 512
    nc = bass.Bass("TRN2", target_bir_lowering=False)

    def din(name, shape, dt=F32):
        return nc.dram_tensor(name, list(shape), dt, kind="ExternalInput").ap()

    x_in = din("x", [S, D])
    nw_in = din("nw", [128, 40])
    fnw_in = din("fnw", [D])
    wqkvg_in = din("wqkvg", [D, 3072])
    wgk1_in = din("wgk1", [D, 16])
    wgk2_in = din("wgk2", [16, 512])
    bgk_in = din("bgk", [128, 4])
    onw_in = din("onw", [256])
    gwo_in = din("gwo", [D, D])
    wkv_in = din("wkv", [D, 2048])
    wq_in = din("wq", [D, D])
    lam_in = din("lam", [256])
    subw_in = din("subw", [128])
    dwo_in = din("dwo", [D, D])
    win_in = din("win", [2, D, 2 * DFF])
    cw_in = din("cw", [2, 128, 44 * 3])
    cb_in = din("cb", [2, 128, 44])
    wout_in = din("wout", [2, DFF, D])
    cst_in = din("cst", [128, 896])
    rope_in = din("rope", [128, 2 * NT * 64])
    out_dram = nc.dram_tensor("out", [S, D], F32, kind="ExternalOutput").ap()
    scratch_kind = "ExternalOutput" if debug else "Internal"
    h_dram = nc.dram_tensor("hscr", [S, D], F32, kind=scratch_kind).ap()
    qt_dram = nc.dram_tensor("qtscr", [NB * 8 * 128, 1024], BF16, kind=scratch_kind).ap()

    with ExitStack() as gst:
        c = Ctx(nc, gst)
        c.limit = limit
        hd_bufs = [c.buf("hd") for _ in range(NT)]
        qt_bufs = [c.buf("qtd") for _ in range(NB * 8)]

        def sb(st, name, shape, dt):
            c.uid += 1
            return st.enter_context(nc.sbuf_tensor("sb%d_%s" % (c.uid, name), list(shape), dt))

        def ps(st, name, shape, dt):
            c.uid += 1
            return st.enter_context(nc.psum_tensor("ps%d_%s" % (c.uid, name), list(shape), dt))

        cst = sb(gst, "cst", [128, 896], F32)
        b_cst = c.buf("cst")
        identb = sb(gst, "identb", [128, 128], BF16)
        b_identb = c.buf("identb")
        trib = sb(gst, "trib", [128, 128], BF16)
        b_trib = c.buf("trib")
        nw = sb(gst, "nw", [128, 40], F32)
        b_nw = c.buf("nw")
        c.dma("sp", cst[:], cst_in[:, :], b_cst, writes=[b_cst])
        c.dma("sp", nw[:], nw_in[:, :], b_nw, writes=[b_nw])
        c.op("dve", lambda e: e.tensor_copy(identb[:], cst[:, 0:128]), reads=[b_cst], writes=[b_identb])
        c.op("dve", lambda e: e.tensor_copy(trib[:], cst[:, 256:384]), reads=[b_cst], writes=[b_trib])
        negmask = sb(gst, "negmask", [128, 128], BF16)
        b_negmask = c.buf("negmask")
        c.op("dve", lambda e: e.tensor_scalar(negmask[:], cst[:, 256:384], -1.0, 30000.0, ALU.add, ALU.mult), reads=[b_cst], writes=[b_negmask])
        glamask = cst[:, 128:256]
        scanmask = cst[:, 384:896]

        cast_rr = [0]

        def load_w(st, dst, b_dst, src, nk, ncols, stg, scale_col=None, cmul=None, col_lo=0, src_col0=0):
            for kc in range(nk):
                c0 = 0
                while c0 < ncols:
                    w = min(2048, ncols - c0)
                    sap, sbuf_ = stg.next()
                    c.dma("sp", sap[:, 0:w], src[kc * 128:(kc + 1) * 128, src_col0 + c0:src_col0 + c0 + w], sbuf_, writes=[sbuf_])
                    use_act = (cast_rr[0] % 2 == 1) and cmul is None
                    cast_rr[0] += 1
                    o = dst[:, kc, col_lo + c0:col_lo + c0 + w]
                    if scale_col is None and cmul is None:
                        if use_act:
                            c.op("act", lambda e, o=o, sap=sap, w=w: e.copy(o, sap[:, 0:w]), reads=[sbuf_], writes=[b_dst])
                        else:
                            c.op("dve", lambda e, o=o, sap=sap, w=w: e.tensor_copy(o, sap[:, 0:w]), reads=[sbuf_], writes=[b_dst])
                    elif scale_col is None:
                        c.op("dve", lambda e, o=o, sap=sap, w=w: e.tensor_scalar(o, sap[:, 0:w], float(cmul), None, ALU.mult),
                             reads=[sbuf_], writes=[b_dst])
                    else:
                        sc = nw[:, scale_col + kc:scale_col + kc + 1]
                        if cmul is None:
                            if use_act:
                                c.op("act", lambda e, o=o, sap=sap, w=w, sc=sc: e.activation(o, sap[:, 0:w], AF.Copy, scale=sc),
                                     reads=[sbuf_, b_nw], writes=[b_dst])
                            else:
                                c.op("dve", lambda e, o=o, sap=sap, w=w, sc=sc: e.tensor_scalar(o, sap[:, 0:w], sc, None, ALU.mult),
                                     reads=[sbuf_, b_nw], writes=[b_dst])
                        else:
                            c.op("dve", lambda e, o=o, sap=sap, w=w, sc=sc: e.tensor_scalar(o, sap[:, 0:w], sc, float(cmul), ALU.mult, ALU.mult),
                                 reads=[sbuf_, b_nw], writes=[b_dst])
                    c0 += w

        def mk_stage(st, n=3):
            items = []
            for i in range(n):
                items.append((sb(st, "stg%d" % i, [128, 2048], F32), c.buf("stg")))
            return Rot(items)

        ssq_glob = {}
        for nm in ("A", "B", "D", "E"):
            ssq_glob[nm] = (sb(gst, "ssq" + nm, [128, NT], F32), c.buf("ssq" + nm))

        def make_norm(st, pre=None, nhnb=2, nht=3):
            hts = Rot([(sb(st, "ht%d" % i, [128, D], F32), c.buf("ht")) for i in range(nht)])
            hnbs = Rot([(sb(st, "hnb%d" % i, [128, D], BF16), c.buf("hnb")) for i in range(nhnb)])
            ncol = NT if pre is not None else 4
            ssq = sb(st, "ssq", [128, 4], F32)
            b_ssq = c.buf("ssq")
            lnv = sb(st, "lnv", [128, ncol], F32)
            b_lnv = c.buf("lnv")
            rstd = sb(st, "rstd", [128, ncol], F32)
            b_rstd = c.buf("rstd")
            if pre is not None:
                sq, b_sq = ssq_glob[pre]
                c.op("act", lambda e: e.activation(lnv[:], sq[:], AF.Ln, bias=EPS, scale=1.0 / D), reads=[b_sq], writes=[b_lnv])
                c.op("act", lambda e: e.activation(rstd[:], lnv[:], AF.Exp, scale=-0.5), reads=[b_lnv], writes=[b_rstd])

            def norm_block(src, src_bufs, blk, hnT, b_hnT, ptr_rot):
                if pre is None:
                    for j in range(4):
                        t = blk * 4 + j
                        ht, b_ht = hts.next()
                        hnb, b_hnb = hnbs.next()
                        c.dma("sp", ht[:], src[t * 128:(t + 1) * 128, :], b_ht, reads=[src_bufs[t]], writes=[b_ht])
                        c.op("act", lambda e, ht=ht, hnb=hnb, j=j: e.activation(hnb[:], ht[:], AF.Square, accum_out=ssq[:, j:j + 1]),
                             reads=[b_ht], writes=[b_hnb, b_ssq])
                    c.op("act", lambda e: e.activation(lnv[:], ssq[:], AF.Ln, bias=EPS, scale=1.0 / D), reads=[b_ssq], writes=[b_lnv])
                    c.op("act", lambda e: e.activation(rstd[:], lnv[:], AF.Exp, scale=-0.5), reads=[b_lnv], writes=[b_rstd])
                for j in range(4):
                    stage1_tile(src, src_bufs, blk, j)
                    stage2_tile(hnT, b_hnT, ptr_rot, j)

            pend = {}

            def stage1_tile(src, src_bufs, blk, j):
                t = blk * 4 + j
                col = t if pre is not None else j
                ht, b_ht = hts.next()
                hnb, b_hnb = hnbs.next()
                c.dma("sp", ht[:], src[t * 128:(t + 1) * 128, :], b_ht, reads=[src_bufs[t]], writes=[b_ht])
                c.op("dve", lambda e: e.tensor_scalar(hnb[:], ht[:], rstd[:, col:col + 1], None, ALU.mult),
                     reads=[b_ht, b_rstd], writes=[b_hnb])
                pend[j] = (hnb, b_hnb)

            def stage2_tile(hnT, b_hnT, ptr_rot, j):
                hnb, b_hnb = pend.pop(j)
                ptr, b_ptr = ptr_rot.next()

                def tr(e):
                    ins = None
                    for kc in range(NKC):
                        ins = e.transpose(ptr[:, kc, :], hnb[:, kc * 128:(kc + 1) * 128], identb[:])
                    return ins
                c.op("pe", tr, reads=[b_hnb, b_identb], writes=[b_ptr])
                c.op("act", lambda e: e.copy(hnT[:, :, j * 128:(j + 1) * 128], ptr[:]), reads=[b_ptr], writes=[b_hnT])

            def stage1(src, src_bufs, blk):
                assert pre is not None and nhnb >= 4
                for j in range(4):
                    stage1_tile(src, src_bufs, blk, j)

            def stage2(hnT, b_hnT, ptr_rot):
                for j in range(4):
                    stage2_tile(hnT, b_hnT, ptr_rot, j)
            norm_block.stage1 = stage1
            norm_block.stage2 = stage2
            return norm_block, hts

        def emit_ssq(name, ht, b_ht, t, junk, b_junk):
            sq, b_sq = ssq_glob[name]
            c.op("act", lambda e: e.activation(junk[:], ht[:], AF.Square, accum_out=sq[:, t:t + 1]), reads=[b_ht], writes=[b_junk, b_sq])

        def phase_A():
            with ExitStack() as st:
                wq = sb(st, "A_wqkvg", [128, NKC, 3072], BF16)
                b_wq = c.buf("A_wqkvg")
                wg1 = sb(st, "A_wg1", [128, NKC, 16], BF16)
                b_wg1 = c.buf("A_wg1")
                wg2 = sb(st, "A_wg2", [16, 512], BF16)
                b_wg2 = c.buf("A_wg2")
                wo = sb(st, "A_wo", [128, NKC, D], BF16)
                b_wo = c.buf("A_wo")
                negb = sb(st, "A_negb", [128, 4], F32)
                b_negb = c.buf("A_negb")
                onw = sb(st, "A_onw", [128, 256], F32)
                b_onw = c.buf("A_onw")
                with ExitStack() as st2:
                    stg = mk_stage(st2)
                    load_w(st2, wq, b_wq, wqkvg_in, NKC, 512, stg, scale_col=0, cmul=128 ** -0.5, col_lo=0, src_col0=0)
                    load_w(st2, wq, b_wq, wqkvg_in, NKC, 2560, stg, scale_col=0, col_lo=512, src_col0=512)
                    load_w(st2, wg1, b_wg1, wgk1_in, NKC, 16, stg, scale_col=0)
                    load_w(st2, wo, b_wo, gwo_in, NKC, D, stg)
                    sap, sbf = stg.next()
                    c.dma("sp", sap[0:16, 0:512], wgk2_in[:, :], sbf, writes=[sbf])
                    c.op("dve", lambda e: e.tensor_copy(wg2[:], sap[0:16, 0:512]), reads=[sbf], writes=[b_wg2])
                    sap2, sbf2 = stg.next()
                    c.dma("sp", sap2[:, 0:4], bgk_in[:, :], sbf2, writes=[sbf2])
                    c.op("dve", lambda e: e.tensor_scalar(negb[:], sap2[:, 0:4], -1.0, None, ALU.mult), reads=[sbf2], writes=[b_negb])
                    c.dma("sp", onw[:], onw_in.partition_broadcast(128), b_onw, writes=[b_onw])
                    c.barrier()
                norm_block, hts = make_norm(st)
                hnT = sb(st, "A_hnT", [128, NKC, 512], BF16)
                b_hnT = c.buf("A_hnT")
                g1s = sb(st, "A_g1s", [16, 512], BF16)
                b_g1s = c.buf("A_g1s")
                e1 = Rot([(sb(st, "A_e1_%d" % i, [128, 512], F32), c.buf("e1")) for i in range(2)])
                cc = Rot([(sb(st, "A_cc_%d" % i, [128, 512], F32), c.buf("cc")) for i in range(2)])
                ebs = Rot([(sb(st, "A_eb_%d" % i, [128, 512], F32), c.buf("eb")) for i in range(2)])
                enbs = Rot([(sb(st, "A_enb_%d" % i, [128, 512], F32), c.buf("enb")) for i in range(2)])
                dds = Rot([(sb(st, "A_dd_%d" % i, [128, 512], F32), c.buf("dd")) for i in range(2)])
                qin = [sb(st, "A_qin%d" % h, [128, 512], BF16) for h in range(4)]
                kin = [sb(st, "A_kin%d" % h, [128, 512], BF16) for h in range(4)]
                qp0 = [sb(st, "A_qp0_%d" % h, [128, 4, 128], BF16) for h in range(4)]
                qp1 = [sb(st, "A_qp1_%d" % h, [128, 4, 128], BF16) for h in range(4)]
                kea = [sb(st, "A_kea%d" % h, [128, 4, 128], BF16) for h in range(4)]
                keb = [sb(st, "A_keb%d" % h, [128, 4, 128], BF16) for h in range(4)]
                dec = [sb(st, "A_dec%d" % h, [128, 8], F32) for h in range(4)]
                b_qin = [c.buf("qin") for _ in range(4)]
                b_kin = [c.buf("kin") for _ in range(4)]
                b_qp0 = [c.buf("qp0") for _ in range(4)]
                b_qp1 = [c.buf("qp1") for _ in range(4)]
                b_kea = [c.buf("kea") for _ in range(4)]
                b_keb = [c.buf("keb") for _ in range(4)]
                b_dec = [c.buf("dec") for _ in range(4)]
                for h in range(4):
                    for tl, bb in ((qp0[h], b_qp0[h]), (qp1[h], b_qp1[h]), (kea[h], b_kea[h]), (keb[h], b_keb[h])):
                        c.op("pool", lambda e, tl=tl: e.memset(tl[:], 0.0), writes=[bb])
                vts = Rot([(sb(st, "A_v%d" % i, [128, D], BF16), c.buf("v")) for i in range(4)])
                gws = Rot([(sb(st, "A_gw%d" % i, [128, D], F32), c.buf("gw")) for i in range(4)])
                osb = Rot([(sb(st, "A_osb%d" % i, [128, D], F32), c.buf("osb")) for i in range(4)])
                ket = Rot([(sb(st, "A_ket%d" % i, [128, 2, 128], BF16), c.buf("ket")) for i in range(3)])
                stm = Rot([(sb(st, "A_stm%d" % i, [128, 128], BF16), c.buf("stm")) for i in range(3)])
                ys = Rot([(sb(st, "A_y%d" % i, [128, D], BF16), c.buf("y")) for i in range(2)])
                yTs = Rot([(sb(st, "A_yT%d" % i, [128, NKC, 128], BF16), c.buf("yT")) for i in range(2)])
                ssqo = sb(st, "A_ssqo", [128, 16], F32)
                b_ssqo = c.buf("ssqo")
                lno = sb(st, "A_lno", [128, 16], F32)
                b_lno = c.buf("lno")
                rso = sb(st, "A_rso", [128, 16], F32)
                b_rso = c.buf("rso")
                junk = sb(st, "A_junk", [128, 256], BF16)
                b_junk = c.buf("junk")
                junk1k = sb(st, "A_junk1k", [128, D], BF16)
                b_junk1k = c.buf("junk1k")
                Sf = [[sb(st, "A_S%d_%d" % (h, i), [128, 256], F32) for i in range(2)] for h in range(4)]
                Sb = [[sb(st, "A_Sb%d_%d" % (h, i), [128, 256], BF16) for i in range(2)] for h in range(4)]
                b_Sf = [[c.buf("Sf") for i in range(2)] for h in range(4)]
                b_Sb = [[c.buf("Sb") for i in range(2)] for h in range(4)]
                for h in range(4):
                    c.op("pool", lambda e, h=h: e.memset(Sf[h][0][:], 0.0), writes=[b_Sf[h][0]])
                    c.op("pool", lambda e, h=h: e.memset(Sb[h][0][:], 0.0), writes=[b_Sb[h][0]])
                mm = Rot([(ps(st, "A_mm%d" % i, [128, 512], F32), c.buf("mm", True)) for i in range(3)])
                ptr_rot = Rot([(ps(st, "A_ptr%d" % i, [128, NKC, 128], BF16), c.buf("ptr", True)) for i in range(1)])
                pst = ps(st, "A_pst", [128, 512], F32)
                b_pstb = c.buf("pst", True)
                pst_rot = Rot([(pst[:, i * 128:(i + 1) * 128], b_pstb) for i in range(4)])
                pco_banks = [(ps(st, "A_pco%d" % i, [128, 512], F32), c.buf("pco", True)) for i in range(2)]
                po = ps(st, "A_po", [128, 512], F32)
                b_pob = c.buf("po", True)
                po_rot = Rot([(po[:, i * 256:(i + 1) * 256], b_pob) for i in range(2)])

                x_bufs = [Buf("xin")] * NT
                norm_block(x_in, x_bufs, 0, hnT, b_hnT, ptr_rot)
                for blk in range(NB):
                    pg, b_pg = mm.next()

                    def mm_g1(e, pg=pg):
                        ins = None
                        for kc in range(NKC):
                            ins = e.matmul(pg[0:16, :], wg1[:, kc, :], hnT[:, kc, :], start=(kc == 0), stop=(kc == NKC - 1))
                        return ins
                    c.op("pe", mm_g1, reads=[b_wg1, b_hnT], writes=[b_pg])
                    c.op("act", lambda e, pg=pg: e.copy(g1s[:], pg[0:16, :]), reads=[b_pg], writes=[b_g1s])
                    for h in range(4):
                        pgk, b_pgk = mm.next()
                        c.op("pe", lambda e, pgk=pgk, h=h: e.matmul(pgk[:], wg2[:, h * 128:(h + 1) * 128], g1s[:], start=True, stop=True),
                             reads=[b_wg2, b_g1s], writes=[b_pgk])
                        e1t, b_e1 = e1.next()
                        cct, b_cc = cc.next()
                        ebt, b_eb = ebs.next()
                        enbt, b_enb = enbs.next()
                        ddt, b_dd = dds.next()
                        c.op("act", lambda e, e1t=e1t, pgk=pgk, h=h: e.activation(e1t[:], pgk[:], AF.Exp, bias=negb[:, h:h + 1], scale=-1.0),
                             reads=[b_pgk, b_negb], writes=[b_e1])
                        c.op("act", lambda e, e1t=e1t: e.activation(e1t[:], e1t[:], AF.Ln, bias=1.0, scale=1.0), reads=[b_e1], writes=[b_e1])
                        c.op("dve", lambda e, cct=cct, e1t=e1t: e.tensor_tensor_scan(cct[:], scanmask, e1t[:], 0.0, ALU.mult, ALU.add),
                             reads=[b_e1, b_cst], writes=[b_cc])
                        c.op("act", lambda e, ebt=ebt, cct=cct: e.activation(ebt[:], cct[:], AF.Exp, scale=-1.0 / 16), reads=[b_cc], writes=[b_eb])
                        c.op("act", lambda e, enbt=enbt, cct=cct: e.activation(enbt[:], cct[:], AF.Exp, scale=1.0 / 16), reads=[b_cc], writes=[b_enb])
                        cc3 = cct[:].rearrange("p (n c) -> p n c", c=64)
                        c.op("pool", lambda e, ddt=ddt, cc3=cc3: e.tensor_tensor(ddt[:].rearrange("p (n c) -> p n c", c=64),
                                                                                  cc3[:, :, 63:64].to_broadcast([128, 8, 64]), cc3, ALU.subtract),
                             reads=[b_cc], writes=[b_dd])
                        c.op("act", lambda e, ddt=ddt: e.activation(ddt[:], ddt[:], AF.Exp, scale=-1.0 / 16), reads=[b_dd], writes=[b_dd])
                        eb3 = ebt[:].rearrange("p (n c) -> p n c", c=64)
                        c.op("pool", lambda e, h=h, eb3=eb3: e.tensor_copy(dec[h][:].unsqueeze(2), eb3[:, :, 63:64]), reads=[b_eb], writes=[b_dec[h]])
                        pq, b_pq = mm.next()

                        def mm_q(e, pq=pq, h=h):
                            ins = None
                            for kc in range(NKC):
                                ins = e.matmul(pq[:], wq[:, kc, h * 128:(h + 1) * 128], hnT[:, kc, :], start=(kc == 0), stop=(kc == NKC - 1))
                            return ins
                        c.op("pe", mm_q, reads=[b_wq, b_hnT], writes=[b_pq])
                        c.op("dve", lambda e, h=h, pq=pq, ebt=ebt: e.tensor_tensor(qin[h][:], pq[:], ebt[:], ALU.mult),
                             reads=[b_pq, b_eb], writes=[b_qin[h]])
                        q4 = qin[h][:].rearrange("p (t c) -> p t c", c=128)
                        c.op("pool", lambda e, h=h, q4=q4: e.tensor_copy(qp0[h][:, :, 0:64], q4[:, :, 0:64]), reads=[b_qin[h]], writes=[b_qp0[h]])
                        c.op("pool", lambda e, h=h, q4=q4: e.tensor_copy(qp1[h][:, :, 64:128], q4[:, :, 64:128]), reads=[b_qin[h]], writes=[b_qp1[h]])
                        pk, b_pk = mm.next()

                        def mm_k(e, pk=pk, h=h):
                            ins = None
                            for kc in range(NKC):
                                ins = e.matmul(pk[:], wq[:, kc, 512 + h * 128:512 + (h + 1) * 128], hnT[:, kc, :], start=(kc == 0), stop=(kc == NKC - 1))
                            return ins
                        c.op("pe", mm_k, reads=[b_wq, b_hnT], writes=[b_pk])
                        c.op("dve", lambda e, h=h, pk=pk, enbt=enbt: e.tensor_tensor(kin[h][:], pk[:], enbt[:], ALU.mult),
                             reads=[b_pk, b_enb], writes=[b_kin[h]])
                        pk4 = pk[:].rearrange("p (t c) -> p t c", c=128)
                        dd4 = ddt[:].rearrange("p (t c) -> p t c", c=128)
                        c.op("dve", lambda e, h=h, pk4=pk4, dd4=dd4: e.tensor_tensor(kea[h][:, :, 0:64], pk4[:, :, 0:64], dd4[:, :, 0:64], ALU.mult),
                             reads=[b_pk, b_dd], writes=[b_kea[h]])
                        c.op("dve", lambda e, h=h, pk4=pk4, dd4=dd4: e.tensor_tensor(keb[h][:, :, 64:128], pk4[:, :, 64:128], dd4[:, :, 64:128], ALU.mult),
                             reads=[b_pk, b_dd], writes=[b_keb[h]])
                    tiles = []
                    for j in range(4):
                        vt, b_vt = vts.next()
                        gw, b_gw = gws.next()
                        for half in range(2):
                            pv, b_pv = mm.next()

                            def mm_v(e, pv=pv, j=j, half=half):
                                ins = None
                                for kc in range(NKC):
                                    ins = e.matmul(pv[:], hnT[:, kc, j * 128:(j + 1) * 128], wq[:, kc, 1024 + half * 512:1024 + (half + 1) * 512],
                                                   start=(kc == 0), stop=(kc == NKC - 1))
                                return ins
                            c.op("pe", mm_v, reads=[b_wq, b_hnT], writes=[b_pv])
                            c.op("dve", lambda e, vt=vt, pv=pv, half=half: e.tensor_copy(vt[:, half * 512:(half + 1) * 512], pv[:]),
                                 reads=[b_pv], writes=[b_vt])
                        tiles.append((vt, b_vt, gw, b_gw))
                    for j in range(4):
                        vt, b_vt, gw, b_gw = tiles[j]
                        for half in range(2):
                            pgm, b_pgm = mm.next()

                            def mm_g(e, pgm=pgm, j=j, half=half):
                                ins = None
                                for kc in range(NKC):
                                    ins = e.matmul(pgm[:], hnT[:, kc, j * 128:(j + 1) * 128], wq[:, kc, 2048 + half * 512:2048 + (half + 1) * 512],
                                                   start=(kc == 0), stop=(kc == NKC - 1))
                                return ins
                            c.op("pe", mm_g, reads=[b_wq, b_hnT], writes=[b_pgm])
                            c.op("act", lambda e, gw=gw, pgm=pgm, half=half: e.activation(gw[:, half * 512:(half + 1) * 512], pgm[:], AF.Silu),
                                 reads=[b_pgm], writes=[b_gw])
                        c.op("dve", lambda e, gw=gw: e.tensor_tensor(gw[:].rearrange("p (h e) -> p h e", h=4), gw[:].rearrange("p (h e) -> p h e", h=4),
                                                                     onw[:].unsqueeze(1).to_broadcast([128, 4, 256]), ALU.mult),
                             reads=[b_gw, b_onw], writes=[b_gw])
                    if blk + 1 < NB:
                        norm_block(x_in, x_bufs, blk + 1, hnT, b_hnT, ptr_rot)
                    osbs = []
                    for j in range(4):
                        ot, b_ot = osb.next()
                        osbs.append((ot, b_ot))

                    def gla_front(i):
                        j, h = divmod(i, 4)
                        vt, b_vt, gw, b_gw = tiles[j]
                        cs = slice(j * 128, (j + 1) * 128)
                        pstt, b_pstt = pst_rot.next()
                        c.op("pe", lambda e: e.matmul(pstt, kin[h][:, cs], qin[h][:, cs], start=True, stop=True),
                             reads=[b_kin[h], b_qin[h]], writes=[b_pstt])
                        smt, b_smt = stm.next()
                        c.op("dve", lambda e: e.tensor_tensor(smt[:], pstt, glamask, ALU.mult),
                             reads=[b_pstt, b_cst], writes=[b_smt])
                        ptr, b_ptr = ptr_rot.next()

                        def tr_k(e):
                            e.transpose(ptr[:, 0, :], kea[h][:, j, :], identb[:])
                            return e.transpose(ptr[:, 1, :], keb[h][:, j, :], identb[:])
                        c.op("pe", tr_k, reads=[b_kea[h], b_keb[h], b_identb], writes=[b_ptr])
                        kt_, b_kt = ket.next()
                        c.op("act", lambda e: e.copy(kt_[:], ptr[:, 0:2, :]), reads=[b_ptr], writes=[b_kt])
                        vh = vt[:, h * 256:(h + 1) * 256]
                        pcb, b_pcb = pco_banks[i % 2]
                        pc0 = pcb[:, 0:256]
                        pc1 = pcb[:, 256:512]
                        c.op("pe", lambda e: e.matmul(pc0, kt_[:, 0, :], vh, start=True, stop=True), reads=[b_kt, b_vt], writes=[b_pcb])
                        c.op("pe", lambda e: e.matmul(pc1, kt_[:, 1, :], vh, start=True, stop=True), reads=[b_kt, b_vt], writes=[b_pcb])
                        return (smt, b_smt, vh, b_vt, pc0, pc1, b_pcb)

                    def gla_back(i, fr):
                        j, h = divmod(i, 4)
                        smt, b_smt, vh, b_vt, pc0, pc1, b_pcb = fr
                        ot, b_ot = osbs[j]
                        n0 = 2 * j
                        c.op("dve", lambda e: e.scalar_tensor_tensor(Sf[h][1][:], Sf[h][0][:], dec[h][:, n0:n0 + 1], pc0, ALU.mult, ALU.add),
                             reads=[b_Sf[h][0], b_dec[h], b_pcb], writes=[b_Sf[h][1]])
                        c.op("act", lambda e: e.copy(Sb[h][1][:], Sf[h][1][:]), reads=[b_Sf[h][1]], writes=[b_Sb[h][1]])
                        pot, b_pot = po_rot.next()

                        def mm_o(e):
                            e.matmul(pot, smt[:], vh, start=True, stop=False)
                            e.matmul(pot, qp0[h][:, j, :], Sb[h][0][:], start=False, stop=False)
                            return e.matmul(pot, qp1[h][:, j, :], Sb[h][1][:], start=False, stop=True)
                        c.op("pe", mm_o, reads=[b_smt, b_vt, b_qp0[h], b_qp1[h], b_Sb[h][0], b_Sb[h][1]], writes=[b_pot])
                        c.op("dve", lambda e: e.scalar_tensor_tensor(Sf[h][0][:], Sf[h][1][:], dec[h][:, n0 + 1:n0 + 2], pc1, ALU.mult, ALU.add),
                             reads=[b_Sf[h][1], b_dec[h], b_pcb], writes=[b_Sf[h][0]])
                        c.op("act", lambda e: e.copy(Sb[h][0][:], Sf[h][0][:]), reads=[b_Sf[h][0]], writes=[b_Sb[h][0]])
                        c.op("dve", lambda e: e.tensor_copy(ot[:, h * 256:(h + 1) * 256], pot), reads=[b_pot], writes=[b_ot])
                        if h == 3:
                            sqt, b_sqt = junk1k, b_junk1k
                            c.op("pool", lambda e: e.tensor_tensor(sqt[:], ot[:], ot[:], ALU.mult), reads=[b_ot], writes=[b_sqt])
                            deferred_red.append((i + 3, lambda: c.op("dve", lambda e: e.tensor_reduce(ssqo[:, j * 4:(j + 1) * 4], sqt[:].rearrange("p (h e) -> p h e", h=4), AX.X, ALU.add),
                                                                     reads=[b_sqt], writes=[b_ssqo])))
                    deferred_red = []

                    def wo_front(j):
                        t = blk * 4 + j
                        c.op("act", lambda e: e.activation(lno[:, j * 4:(j + 1) * 4], ssqo[:, j * 4:(j + 1) * 4], AF.Ln, bias=EPS, scale=1.0 / 256),
                             reads=[b_ssqo], writes=[b_lno])
                        c.op("act", lambda e: e.activation(rso[:, j * 4:(j + 1) * 4], lno[:, j * 4:(j + 1) * 4], AF.Exp, scale=-0.5), reads=[b_lno], writes=[b_rso])
                        vt, b_vt, gw, b_gw = tiles[j]
                        ot, b_ot = osbs[j]
                        yt, b_yt = ys.next()
                        for h in range(4):
                            col = j * 4 + h
                            hs = slice(h * 256, (h + 1) * 256)
                            c.op("dve", lambda e, hs=hs, col=col: e.scalar_tensor_tensor(yt[:, hs], ot[:, hs], rso[:, col:col + 1], gw[:, hs], ALU.mult, ALU.mult),
                                 reads=[b_ot, b_rso, b_gw], writes=[b_yt])
                        ptr, b_ptr = ptr_rot.next()

                        def tr_y(e):
                            ins = None
                            for kc in range(NKC):
                                ins = e.transpose(ptr[:, kc, :], yt[:, kc * 128:(kc + 1) * 128], identb[:])
                            return ins
                        c.op("pe", tr_y, reads=[b_yt, b_identb], writes=[b_ptr])
                        yT, b_yT = yTs.next()
                        c.op("act", lambda e: e.copy(yT[:], ptr[:]), reads=[b_ptr], writes=[b_yT])
                        ht, b_ht = hts.next()
                        c.dma("sp", ht[:], x_in[t * 128:(t + 1) * 128, :], b_ht, writes=[b_ht])
                        return (t, yT, b_yT, ht, b_ht, yt, b_yt)

                    def wo_back(fr):
                        t, yT, b_yT, ht, b_ht, yt, b_yt = fr
                        for half in range(2):
                            pw, b_pw = mm.next()

                            def mm_wo(e, pw=pw, half=half):
                                ins = None
                                for kc in range(NKC):
                                    ins = e.matmul(pw[:], yT[:, kc, :], wo[:, kc, half * 512:(half + 1) * 512], start=(kc == 0), stop=(kc == NKC - 1))
                                return ins
                            c.op("pe", mm_wo, reads=[b_yT, b_wo], writes=[b_pw])
                            c.op("dve", lambda e, pw=pw, half=half: e.tensor_tensor(ht[:, half * 512:(half + 1) * 512], ht[:, half * 512:(half + 1) * 512], pw[:], ALU.add),
                                 reads=[b_pw, b_ht], writes=[b_ht])
                        emit_ssq("A", ht, b_ht, t, yt, b_yt)
                        c.dma("sp", h_dram[t * 128:(t + 1) * 128, :], ht[:], b_ht, reads=[b_ht], writes=[hd_bufs[t]])
                    sched = {}
                    for j in range(4):
                        sched.setdefault(4 * j + 9, []).append(("f", j))
                        sched.setdefault(4 * j + 11, []).append(("b", j))
                    wfr = {}

                    def run_ev(ev):
                        kind, j = ev
                        if kind == "f":
                            wfr[j] = wo_front(j)
                        else:
                            wo_back(wfr.pop(j))
                    cur_f = gla_front(0)
                    for i in range(16):
                        nx_f = gla_front(i + 1) if i + 1 < 16 else None
                        gla_back(i, cur_f)
                        cur_f = nx_f
                        while deferred_red and deferred_red[0][0] <= i:
                            deferred_red.pop(0)[1]()
                        for ev in sched.pop(i, []):
                            run_ev(ev)
                    while deferred_red:
                        deferred_red.pop(0)[1]()
                    for k in sorted(sched):
                        for ev in sched[k]:
                            run_ev(ev)
                c.barrier()


        def phase_F(l, ssq_in, ssq_out):
            with ExitStack() as st:
                win = sb(st, "F_win", [128, NKC, 2 * DFF], BF16)
                b_wing = [c.buf("F_win") for _ in range(22)]
                wout = sb(st, "F_wout", [128, NFC, D], BF16)
                b_woutc = [c.buf("F_wout") for _ in range(NFC)]
                cw = sb(st, "F_cw", [128, 132], F32)
                b_cw = c.buf("F_cw")
                cb = sb(st, "F_cb", [128, 44], F32)
                b_cb = c.buf("F_cb")
                c.dma("sp", cw[:], cw_in[l, :, :], b_cw, writes=[b_cw])
                c.dma("sp", cb[:], cb_in[l, :, :], b_cb, writes=[b_cb])
                sc_col = 8 if l == 0 else 32
                wflat32 = wout[:].rearrange("p f c -> p (f c)").bitcast(F32)
                NSTG = 3
                STG0 = 10
                stg_bufs = [c.buf("wstg") for _ in range(NSTG)]
                stg_n = [0]
                win_src = win_in[l].rearrange("(k p) c -> p k c", p=128)
                wgroups = []
                for m in range(11):
                    wgroups += [m, 11 + m]
                wstate = {"g": 0, "o": 0}

                def load_win_group():
                    if wstate["g"] >= 22:
                        return
                    g = wgroups[wstate["g"]]
                    wstate["g"] += 1
                    i = stg_n[0] % NSTG
                    stg_n[0] += 1
                    sv = wflat32[:, STG0 * 512 + i * 2048:STG0 * 512 + (i + 1) * 2048].rearrange("p (k c) -> p k c", k=8)
                    c.dma("sp", sv, win_src[:, :, g * 256:(g + 1) * 256], stg_bufs[i], writes=[stg_bufs[i]])
                    if wstate["g"] % 2 == 0:
                        def cast_act(e):
                            ins = None
                            for kc in range(NKC):
                                ins = e.activation(win[:, kc, g * 256:(g + 1) * 256], sv[:, kc, :], AF.Copy, scale=nw[:, sc_col + kc:sc_col + kc + 1])
                            return ins
                        c.op("act", cast_act, reads=[stg_bufs[i], b_nw], writes=[b_wing[g]])
                    else:
                        c.op("dve", lambda e: e.tensor_tensor(win[:, :, g * 256:(g + 1) * 256], sv,
                                                              nw[:, sc_col:sc_col + 8].unsqueeze(2).to_broadcast([128, 8, 256]), ALU.mult),
                             reads=[stg_bufs[i], b_nw], writes=[b_wing[g]])

                def load_wout_chunk():
                    fc = wstate["o"]
                    if fc >= NFC:
                        return
                    if fc >= STG0 and wstate["g"] < 22:
                        return
                    wstate["o"] += 1
                    ht, b_ht = hts.next()
                    c.dma("sp", ht[:], wout_in[l][fc * 128:(fc + 1) * 128, :], b_ht, writes=[b_ht])
                    wr = [b_woutc[fc]]
                    if fc >= STG0:
                        wr.append(stg_bufs[(fc - STG0) // 4])
                    c.op("act", lambda e: e.copy(wout[:, fc, :], ht[:]), reads=[b_ht], writes=wr)
                norm_block, hts = make_norm(st, pre=ssq_in, nhnb=4, nht=2)
                hnT = sb(st, "F_hnT", [128, NKC, 512], BF16)
                b_hnT = c.buf("F_hnT")
                actT = sb(st, "F_actT", [128, NFC, 512], BF16)
                b_actT = [c.buf("actT") for _ in range(NFC)]
                Us = {r: Rot([(sb(st, "F_U%s%d" % (r, i), [128, 514], F32), c.buf("U"), c.buf("Uh")) for i in range(2)]) for r in "ag"}
                Xs = {r: Rot([(sb(st, "F_X%s%d" % (r, i), [128, 512], F32), c.buf("X")) for i in range(3)]) for r in "ag"}
                halo = sb(st, "F_halo", [128, 44, 2], F32)
                b_halo = [c.buf("halo") for _ in range(44)]
                c.op("pool", lambda e: e.memset(halo[:], 0.0), writes=b_halo)
                junk = sb(st, "F_junk", [128, D], BF16)
                b_junk = c.buf("junk")
                mm = Rot([(ps(st, "F_mm%d" % i, [128, 512], F32), c.buf("mm", True)) for i in range(4)])
                ptr_rot = Rot([(ps(st, "F_ptr%d" % i, [128, NKC, 128], BF16), c.buf("ptr", True)) for i in range(1)])
                wo_rot = Rot([(ps(st, "F_wo%d" % i, [128, 512], F32), c.buf("wo", True)) for i in range(3)])
                norm_block(h_dram, hd_bufs, 0, hnT, b_hnT, ptr_rot)
                for _ in range(4):
                    load_win_group()
                for blk in range(NB):
                    def ffn_front(cp):
                        xr = {}
                        for role, ch in (("a", cp), ("g", NFC + cp)):
                            pm, b_pm = mm.next()

                            def mm_in(e, pm=pm, ch=ch):
                                ins = None
                                for kc in range(NKC):
                                    ins = e.matmul(pm[:], win[:, kc, ch * 128:(ch + 1) * 128], hnT[:, kc, :], start=(kc == 0), stop=(kc == NKC - 1))
                                return ins
                            c.op("pe", mm_in, reads=[b_wing[ch // 2], b_hnT], writes=[b_pm])
                            U, b_U, b_Uh = Us[role].next()
                            X, b_X = Xs[role].next()
                            c.op("pool", lambda e, U=U, ch=ch: e.tensor_copy(U[:, 0:2], halo[:, ch, :]), reads=[b_halo[ch]], writes=[b_Uh])
                            c.op("act", lambda e, U=U, pm=pm: e.copy(U[:, 2:514], pm[:]), reads=[b_pm], writes=[b_U])
                            c.op("act", lambda e, X=X, pm=pm, ch=ch: e.activation(X[:], pm[:], AF.Identity, bias=cb[:, ch:ch + 1], scale=cw[:, ch * 3 + 2:ch * 3 + 3]),
                                 reads=[b_pm, b_cw, b_cb], writes=[b_X])
                            c.op("pool", lambda e, U=U, ch=ch: e.tensor_copy(halo[:, ch, :], U[:, 512:514]), reads=[b_U], writes=[b_halo[ch]])
                            c.op("dve", lambda e, X=X, U=U, ch=ch: e.scalar_tensor_tensor(X[:], U[:, 1:513], cw[:, ch * 3 + 1:ch * 3 + 2], X[:], ALU.mult, ALU.add),
                                 reads=[b_U, b_Uh, b_X, b_cw], writes=[b_X])
                            c.op("dve", lambda e, X=X, U=U, ch=ch: e.scalar_tensor_tensor(X[:], U[:, 0:512], cw[:, ch * 3:ch * 3 + 1], X[:], ALU.mult, ALU.add),
                                 reads=[b_U, b_Uh, b_X, b_cw], writes=[b_X])
                            xr[role] = (X, b_X)
                        return xr

                    def ffn_back(cp, xr):
                        Xa, b_Xa = xr["a"]
                        Xg, b_Xg = xr["g"]
                        c.op("act", lambda e: e.activation(Xg[:], Xg[:], AF.Silu), reads=[b_Xg], writes=[b_Xg])
                        c.op("pool", lambda e: e.tensor_tensor(actT[:, cp, :], Xa[:], Xg[:], ALU.mult),
                             reads=[b_Xa, b_Xg], writes=[b_actT[cp]])
                    cur = ffn_front(0)
                    for cp in range(NFC):
                        if blk == 0:
                            if cp % 2 == 0:
                                load_win_group()
                                load_win_group()
                            load_wout_chunk()
                        if cp == 17 and blk + 1 < NB:
                            norm_block.stage1(h_dram, hd_bufs, blk + 1)
                        nx = ffn_front(cp + 1) if cp + 1 < NFC else None
                        ffn_back(cp, cur)
                        cur = nx
                    if blk == 0:
                        while wstate["g"] < 22:
                            load_win_group()
                        while wstate["o"] < NFC:
                            load_wout_chunk()
                    NE = 16 if blk > 0 else 8
                    groups = [(j, half) for j in range(4) for half in range(2)]
                    banks = list(wo_rot.items) + list(mm.items)
                    nearly = min(len(banks), len(groups))

                    def mm_part(pw, j, half, f0, f1):
                        def fn(e):
                            ins = None
                            for fc in range(f0, f1):
                                ins = e.matmul(pw[:], actT[:, fc, j * 128:(j + 1) * 128], wout[:, fc, half * 512:(half + 1) * 512],
                                               start=(fc == 0), stop=(fc == NFC - 1))
                            return ins
                        return fn
                    for gi in range(nearly):
                        j, half = groups[gi]
                        pw, b_pw = banks[gi]
                        c.op("pe", mm_part(pw, j, half, 0, NE), reads=b_actT[0:NE] + b_woutc[0:NE], writes=[b_pw])
                    if blk + 1 < NB:
                        norm_block.stage2(hnT, b_hnT, ptr_rot)
                    ht_cur = None
                    for gi, (j, half) in enumerate(groups):
                        t = blk * 4 + j
                        if half == 0:
                            ht, b_ht = hts.next()
                            c.dma("sp", ht[:], h_dram[t * 128:(t + 1) * 128, :], b_ht, reads=[hd_bufs[t]], writes=[b_ht])
                            ht_cur = (ht, b_ht)
                        ht, b_ht = ht_cur
                        if gi < nearly:
                            pw, b_pw = banks[gi]
                            c.op("pe", mm_part(pw, j, half, NE, NFC), reads=b_actT[NE:] + b_woutc[NE:], writes=[b_pw])
                        else:
                            pw, b_pw = banks[gi - nearly]
                            c.op("pe", mm_part(pw, j, half, 0, NFC), reads=b_actT + b_woutc, writes=[b_pw])
                        c.op("dve", lambda e, ht=ht, pw=pw, half=half: e.tensor_tensor(ht[:, half * 512:(half + 1) * 512], ht[:, half * 512:(half + 1) * 512], pw[:], ALU.add),
                             reads=[b_pw, b_ht], writes=[b_ht])
                        if half == 1:
                            emit_ssq(ssq_out, ht, b_ht, t, junk, b_junk)
                            c.dma("sp", h_dram[t * 128:(t + 1) * 128, :], ht[:], b_ht, reads=[b_ht], writes=[hd_bufs[t]])
                c.barrier()

        def make_rope(st):
            cosr = Rot([(sb(st, "cos%d" % i, [128, 64], F32), c.buf("cos")) for i in range(2)])
            sinr = Rot([(sb(st, "sin%d" % i, [128, 64], F32), c.buf("sin")) for i in range(2)])
            t1r = Rot([(sb(st, "t1_%d" % i, [128, D], F32), c.buf("t1")) for i in range(2)])
            t2r = Rot([(sb(st, "t2_%d" % i, [128, D], F32), c.buf("t2")) for i in range(2)])

            def rope(xs, b_xs, t, outb, b_outb):
                cs, b_cs = cosr.next()
                sn, b_sn = sinr.next()
                c.dma("sp", cs[:], rope_in[:, t * 64:(t + 1) * 64], b_cs, writes=[b_cs])
                c.dma("sp", sn[:], rope_in[:, NT * 64 + t * 64:NT * 64 + (t + 1) * 64], b_sn, writes=[b_sn])
                t1, b_t1 = t1r.next()
                t2, b_t2 = t2r.next()
                x3 = xs[:].rearrange("p (g d) -> p g d", d=64)
                t13 = t1[:].rearrange("p (g d) -> p g d", d=64)
                t23 = t2[:].rearrange("p (g d) -> p g d", d=64)
                c.op("dve", lambda e: e.tensor_tensor(t13, x3, cs[:].unsqueeze(1).to_broadcast([128, 16, 64]), ALU.mult),
                     reads=[b_xs, b_cs], writes=[b_t1])
                c.op("pool", lambda e: e.tensor_tensor(t23[:, :, 0:32], x3[:, :, 32:64], sn[:, 0:32].unsqueeze(1).to_broadcast([128, 16, 32]), ALU.mult),
                     reads=[b_xs, b_sn], writes=[b_t2])
                c.op("pool", lambda e: e.tensor_tensor(t23[:, :, 32:64], x3[:, :, 0:32], sn[:, 32:64].unsqueeze(1).to_broadcast([128, 16, 32]), ALU.mult),
                     reads=[b_xs, b_sn], writes=[b_t2])
                c.op("dve", lambda e: e.tensor_tensor(outb[:], t1[:], t2[:], ALU.add), reads=[b_t1, b_t2], writes=[b_outb])
            return rope

        def phase_QV(VE, b_VE):
            with ExitStack() as st:
                wq = sb(st, "Q_wq", [128, NKC, D], BF16)
                b_wq = c.buf("Q_wq")
                wv = sb(st, "Q_wv", [128, NKC, D], BF16)
                b_wv = c.buf("Q_wv")
                with ExitStack() as st2:
                    stg = mk_stage(st2)
                    load_w(st2, wq, b_wq, wq_in, NKC, D, stg, scale_col=24, cmul=64 ** -0.5)
                    load_w(st2, wv, b_wv, wkv_in, NKC, D, stg, scale_col=16, src_col0=1024)
                    c.barrier()
                c.op("pool", lambda e: e.memset(VE[:], 1.0), writes=[b_VE])
                norm_block, hts = make_norm(st, pre="B")
                rope = make_rope(st)
                hnT = sb(st, "Q_hnT", [128, NKC, 512], BF16)
                b_hnT = c.buf("Q_hnT")
                xsr = Rot([(sb(st, "Q_xs%d" % i, [128, D], F32), c.buf("xs")) for i in range(2)])
                qbr = Rot([(sb(st, "Q_qb%d" % i, [128, D], BF16), c.buf("qb")) for i in range(2)])
                qst = sb(st, "Q_qst", [128, 8, 2, 512], BF16)
                b_qst = c.buf("qst")
                c.op("pool", lambda e: e.memset(qst[:], 0.0), writes=[b_qst])
                mm = Rot([(ps(st, "Q_mm%d" % i, [128, 512], F32), c.buf("mm", True)) for i in range(4)])
                ptr_rot = Rot([(ps(st, "Q_ptr%d" % i, [128, NKC, 128], BF16), c.buf("ptr", True)) for i in range(2)])
                pend_q = []
                norm_block(h_dram, hd_bufs, 0, hnT, b_hnT, ptr_rot)
                for blk in range(NB):
                    for j in range(4):
                        t = blk * 4 + j
                        xs, b_xs = xsr.next()
                        for half in range(2):
                            pq, b_pq = mm.next()

                            def mm_q(e, pq=pq, j=j, half=half):
                                ins = None
                                for kc in range(NKC):
                                    ins = e.matmul(pq[:], hnT[:, kc, j * 128:(j + 1) * 128], wq[:, kc, half * 512:(half + 1) * 512],
                                                   start=(kc == 0), stop=(kc == NKC - 1))
                                return ins
                            c.op("pe", mm_q, reads=[b_wq, b_hnT], writes=[b_pq])
                            c.op("act", lambda e, xs=xs, pq=pq, half=half: e.copy(xs[:, half * 512:(half + 1) * 512], pq[:]), reads=[b_pq], writes=[b_xs])
                        for half in range(2):
                            pv, b_pv = mm.next()

                            def mm_v(e, pv=pv, j=j, half=half):
                                ins = None
                                for kc in range(NKC):
                                    ins = e.matmul(pv[:], hnT[:, kc, j * 128:(j + 1) * 128], wv[:, kc, half * 512:(half + 1) * 512],
                                                   start=(kc == 0), stop=(kc == NKC - 1))
                                return ins
                            c.op("pe", mm_v, reads=[b_wv, b_hnT], writes=[b_pv])
                            c.op("dve", lambda e, pv=pv, t=t, half=half: e.tensor_copy(VE[:, t, half * 4:(half + 1) * 4, 0:128], pv[:].rearrange("p (h e) -> p h e", e=128)),
                                 reads=[b_pv], writes=[b_VE])
                        qb_, b_qb = qbr.next()
                        rope(xs, b_xs, t, qb_, b_qb)

                        def finish_q(qb_=qb_, b_qb=b_qb, j=j):
                            ptr, b_ptr = ptr_rot.next()

                            def tr_q(e):
                                ins = None
                                for hh in range(8):
                                    ins = e.transpose(ptr[:, hh, :], qb_[:, hh * 128:(hh + 1) * 128], identb[:])
                                return ins
                            c.op("pe", tr_q, reads=[b_qb, b_identb], writes=[b_ptr])
                            c.op("act", lambda e: e.copy(qst[0:64, :, 0, j * 128:(j + 1) * 128], ptr[0:64, :, :]), reads=[b_ptr], writes=[b_qst])
                            c.op("act", lambda e: e.copy(qst[64:128, :, 1, j * 128:(j + 1) * 128], ptr[64:128, :, :]), reads=[b_ptr], writes=[b_qst])
                        if pend_q:
                            pend_q.pop()()
                        pend_q.append(finish_q)
                    if blk + 1 < NB:
                        norm_block(h_dram, hd_bufs, blk + 1, hnT, b_hnT, ptr_rot)
                    pend_q.pop()()
                    c.dma("sp", qt_dram[blk * 1024:(blk + 1) * 1024, :].rearrange("(h p) c -> p h c", p=128), qst[:].rearrange("p h s t -> p h (s t)"),
                          b_qst, reads=[b_qst], writes=qt_bufs[blk * 8:(blk + 1) * 8])
                c.barrier()

        def phase_K(KT, b_KT):
            with ExitStack() as st:
                wk = sb(st, "K_wk", [128, NKC, D], BF16)
                b_wk = c.buf("K_wk")
                with ExitStack() as st2:
                    stg = mk_stage(st2)
                    load_w(st2, wk, b_wk, wkv_in, NKC, D, stg, scale_col=16, src_col0=0)
                    c.barrier()
                norm_block, hts = make_norm(st, pre="B")
                rope = make_rope(st)
                hnT = sb(st, "K_hnT", [128, NKC, 512], BF16)
                b_hnT = c.buf("K_hnT")
                xsr = Rot([(sb(st, "K_xs%d" % i, [128, D], F32), c.buf("xs")) for i in range(2)])
                kbr = Rot([(sb(st, "K_kb%d" % i, [128, D], BF16), c.buf("kb")) for i in range(2)])
                mm = Rot([(ps(st, "K_mm%d" % i, [128, 512], F32), c.buf("mm", True)) for i in range(4)])
                ptr_rot = Rot([(ps(st, "K_ptr%d" % i, [128, NKC, 128], BF16), c.buf("ptr", True)) for i in range(2)])
                pend_k = []
                norm_block(h_dram, hd_bufs, 0, hnT, b_hnT, ptr_rot)
                for blk in range(NB):
                    for j in range(4):
                        t = blk * 4 + j
                        xs, b_xs = xsr.next()
                        for half in range(2):
                            pk_, b_pk = mm.next()

                            def mm_k(e, pk_=pk_, j=j, half=half):
                                ins = None
                                for kc in range(NKC):
                                    ins = e.matmul(pk_[:], hnT[:, kc, j * 128:(j + 1) * 128], wk[:, kc, half * 512:(half + 1) * 512],
                                                   start=(kc == 0), stop=(kc == NKC - 1))
                                return ins
                            c.op("pe", mm_k, reads=[b_wk, b_hnT], writes=[b_pk])
                            c.op("act", lambda e, xs=xs, pk_=pk_, half=half: e.copy(xs[:, half * 512:(half + 1) * 512], pk_[:]), reads=[b_pk], writes=[b_xs])
                        kb_, b_kb = kbr.next()
                        rope(xs, b_xs, t, kb_, b_kb)

                        def finish_k(kb_=kb_, b_kb=b_kb, t=t):
                            ptr, b_ptr = ptr_rot.next()

                            def tr_k(e):
                                ins = None
                                for hh in range(8):
                                    ins = e.transpose(ptr[:, hh, :], kb_[:, hh * 128:(hh + 1) * 128], identb[:])
                                return ins
                            c.op("pe", tr_k, reads=[b_kb, b_identb], writes=[b_ptr])
                            c.op("act", lambda e: e.copy(KT[:, :, t * 128:(t + 1) * 128], ptr[:]), reads=[b_ptr], writes=b_KT)
                        if pend_k:
                            pend_k.pop()()
                        pend_k.append(finish_k)
                    if blk + 1 < NB:
                        norm_block(h_dram, hd_bufs, blk + 1, hnT, b_hnT, ptr_rot)
                    pend_k.pop()()
                c.barrier()

        def phase_D(KT, b_KT, VE, b_VE):
            with ExitStack() as st:
                wo = sb(st, "D_wo", [128, NKC, D], BF16)
                b_wo = c.buf("D_wo")
                lamt = sb(st, "D_lamt", [128, 256], F32)
                b_lamt = c.buf("lamt")
                lsc = sb(st, "D_lsc", [128, 8], F32)
                b_lsc = c.buf("lsc")
                subw = sb(st, "D_subw", [128, 128], F32)
                b_subw = c.buf("subw")
                wo_state = {"kc": 0}

                def load_wo_chunk():
                    kc = wo_state["kc"]
                    if kc >= NKC:
                        return
                    wo_state["kc"] += 1
                    ht, b_ht = hts.next()
                    c.dma("sp", ht[:], dwo_in[kc * 128:(kc + 1) * 128, :], b_ht, writes=[b_ht])
                    c.op("dve", lambda e: e.tensor_copy(wo[:, kc, :], ht[:]), reads=[b_ht], writes=[b_wo])
                c.dma("sp", lamt[:], lam_in.partition_broadcast(128), b_lamt, writes=[b_lamt])
                c.dma("sp", subw[:], subw_in.partition_broadcast(128), b_subw, writes=[b_subw])
                c.op("dve", lambda e: e.tensor_scalar(subw[:], subw[:], 1.0 - LAM_INIT, None, ALU.mult), reads=[b_subw], writes=[b_subw])
                l4 = lamt[:].rearrange("p (a b) -> p a b", b=64)
                c.op("dve", lambda e: e.tensor_tensor(l4[:, 0:1, :], l4[:, 0:1, :], l4[:, 1:2, :], ALU.mult), reads=[b_lamt], writes=[b_lamt])
                c.op("dve", lambda e: e.tensor_tensor(l4[:, 2:3, :], l4[:, 2:3, :], l4[:, 3:4, :], ALU.mult), reads=[b_lamt], writes=[b_lamt])
                c.op("dve", lambda e: e.tensor_reduce(lsc[:, 0:1], lamt[:, 0:64], AX.X, ALU.add), reads=[b_lamt], writes=[b_lsc])
                c.op("dve", lambda e: e.tensor_reduce(lsc[:, 1:2], lamt[:, 128:192], AX.X, ALU.add), reads=[b_lamt], writes=[b_lsc])
                c.op("act", lambda e: e.activation(lsc[:, 2:4], lsc[:, 0:2], AF.Exp), reads=[b_lsc], writes=[b_lsc])
                c.op("dve", lambda e: e.tensor_tensor(lsc[:, 4:5], lsc[:, 3:4], lsc[:, 2:3], ALU.subtract), reads=[b_lsc], writes=[b_lsc])
                c.op("dve", lambda e: e.tensor_scalar(lsc[:, 5:6], lsc[:, 4:5], -LAM_INIT, None, ALU.add), reads=[b_lsc], writes=[b_lsc])
                nlam = lsc[:, 5:6]
                hts = Rot([(sb(st, "D_ht%d" % i, [128, D], F32), c.buf("ht")) for i in range(3)])
                qtr = Rot([(sb(st, "D_qt%d" % i, [128, 2, 512], BF16), c.buf("qt")) for i in range(3)])
                ptr_ = Rot([(sb(st, "D_pt%d" % i, [128, 512], BF16), c.buf("pt")) for i in range(4)])
                eOr = Rot([(sb(st, "D_eO%d" % i, [128, 2, 512], F32), c.buf("eO")) for i in range(2)])
                eSr = Rot([(sb(st, "D_eS%d" % i, [128, 2, 512], F32), c.buf("eS")) for i in range(2)])
                osqr = Rot([(sb(st, "D_osq%d" % i, [128, 512], BF16), c.buf("osq")) for i in range(2)])
                yT = sb(st, "D_yT", [128, 8, 512], BF16)
                b_yT = [c.buf("yT") for _ in range(8)]
                onesb = sb(st, "D_ones", [128, 128], BF16)
                b_onesb = c.buf("ones")
                c.op("pool", lambda e: e.memset(onesb[:], 1.0), writes=[b_onesb])
                subwc = sb(st, "D_subwc", [128, 1], F32)
                b_subwc = c.buf("subwc")
                c.dma("sp", subwc[:], subw_in.rearrange("(p o) -> p o", o=1), b_subwc, writes=[b_subwc])
                c.op("dve", lambda e: e.tensor_scalar(subwc[:], subwc[:], 1.0 - LAM_INIT, None, ALU.mult), reads=[b_subwc], writes=[b_subwc])
                pending_epi = []
                head_res = {}
                pending_wo = []
                pending_red = []
                junk = sb(st, "D_junk", [128, D], BF16)
                b_junk = c.buf("junk")
                stp = Rot([(ps(st, "D_st%d" % i, [128, 512], F32), c.buf("st", True)) for i in range(3)])
                accO = [(ps(st, "D_accO%d" % i, [128, 512], F32), c.buf("accO", True)) for i in range(2)]
                accS = [(ps(st, "D_accS%d" % i, [128, 512], F32), c.buf("accS", True)) for i in range(2)]
                pwr = Rot([(ps(st, "D_pw%d" % i, [128, 512], F32), c.buf("pw", True)) for i in range(1)])
                for qb in range(NB):
                    nkt = 4 * qb + 4
                    items = [(h, comp, kt) for h in range(8) for comp in range(2) for kt in range(nkt)]
                    def head_ctx(h, qb=qb):
                        if (qb, h) not in head_res:
                            qt, b_qt = qtr.next()
                            c.dma("sp", qt[:].rearrange("p s t -> p (s t)"), qt_dram[(qb * 8 + h) * 128:(qb * 8 + h + 1) * 128, :], b_qt,
                                  reads=[qt_bufs[qb * 8 + h]], writes=[b_qt])
                            eO, b_eO = eOr.next()
                            eS, b_eS = eSr.next()
                            head_res[(qb, h)] = (qt, b_qt, eO, b_eO, eS, b_eS)
                        return head_res[(qb, h)]

                    def emit_qk(idx):
                        h, comp, kt = items[idx]
                        qt, b_qt = head_ctx(h)[0:2]
                        o = kt * 128 - qb * 512
                        lo = max(o, 0)
                        pst, b_pst = stp.next()
                        if o >= 0:
                            def qk(e):
                                e.matmul(pst[:, lo:512], KT[:, h, kt * 128:(kt + 1) * 128], qt[:, comp, lo:512], start=True, stop=False)
                                return e.matmul(pst[:, o:o + 128], identb[:], negmask[:], start=False, stop=True)
                            c.op("pe", qk, reads=[b_KT[h], b_qt, b_identb, b_negmask], writes=[b_pst])
                        else:
                            c.op("pe", lambda e: e.matmul(pst[:, lo:512], KT[:, h, kt * 128:(kt + 1) * 128], qt[:, comp, lo:512], start=True, stop=True),
                                 reads=[b_KT[h], b_qt], writes=[b_pst])
                        return pst, b_pst
                    LOOK = 2
                    n_head = 2 * nkt
                    epi_slots = (1, nkt)
                    inflight = [emit_qk(i) for i in range(min(LOOK, len(items)))]
                    for idx, (h, comp, kt) in enumerate(items):
                        qt, b_qt, eO, b_eO, eS, b_eS = head_ctx(h)
                        aO, b_aO = accO[comp]
                        aS, b_aS = accS[comp]
                        o = kt * 128 - qb * 512
                        lo = max(o, 0)
                        pst, b_pst = inflight.pop(0)
                        pt, b_pt = ptr_.next()
                        c.op("act", lambda e, pt=pt, pst=pst, lo=lo: e.activation(pt[:, lo:512], pst[:, lo:512], AF.Exp), reads=[b_pst], writes=[b_pt])
                        if idx + LOOK < len(items):
                            inflight.append(emit_qk(idx + LOOK))
                        rel = comp * nkt + kt
                        if rel == 2:
                            if h < 7:
                                head_ctx(h + 1)
                            elif qb + 1 < NB:
                                head_ctx(0, qb + 1)
                        if rel in epi_slots and pending_epi:
                            pending_epi.pop(0)()
                        if qb == 0 and idx % 4 == 2:
                            load_wo_chunk()
                        if idx >= nkt + 1 and pending_wo:
                            pending_wo.pop(0)()
                        first = (kt == 0)
                        last = (kt == nkt - 1)
                        c.op("pe", lambda e, aO=aO, pt=pt, kt=kt, h=h, lo=lo, first=first, last=last: e.matmul(aO[:, lo:512], VE[:, kt, h, 0:128], pt[:, lo:512], start=first, stop=last),
                             reads=[b_pt, b_VE], writes=[b_aO])
                        c.op("pe", lambda e, aS=aS, pt=pt, lo=lo, first=first, last=last: e.matmul(aS[:, lo:512], onesb[:], pt[:, lo:512], start=first, stop=last),
                             reads=[b_pt, b_onesb], writes=[b_aS])
                        if last:
                            c.op("dve", lambda e, aS=aS, comp=comp, eS=eS: e.reciprocal(eS[:, comp, :], aS[:]), reads=[b_aS], writes=[b_eS])
                            if comp == 0:
                                c.op("dve", lambda e, aO=aO, eO=eO, eS=eS: e.tensor_tensor(eO[:, 0, :], aO[:], eS[:, 0, :], ALU.mult),
                                     reads=[b_aO, b_eS], writes=[b_eO])
                            else:
                                c.op("dve", lambda e, aO=aO, eO=eO, eS=eS: e.scalar_tensor_tensor(eO[:, 1, :], aO[:], nlam, eS[:, 1, :], ALU.mult, ALU.mult),
                                     reads=[b_aO, b_eS, b_lsc], writes=[b_eO])
                        if last and comp == 1:
                            def epilogue(eO=eO, b_eO=b_eO, eS=eS, b_eS=b_eS, h=h):
                                osq, b_osq = osqr.next()
                                pss, b_pss = pwr.next()

                                def p1():
                                    c.op("dve", lambda e: e.tensor_tensor(eO[:, 0, :], eO[:, 0, :], eO[:, 1, :], ALU.add), reads=[b_eO], writes=[b_eO])
                                    c.op("dve", lambda e: e.tensor_tensor(osq[:], eO[:, 0, :], eO[:, 0, :], ALU.mult), reads=[b_eO], writes=[b_osq])

                                def p2():
                                    c.op("pe", lambda e: e.matmul(pss[:], onesb[:], osq[:], start=True, stop=True), reads=[b_osq, b_onesb], writes=[b_pss])

                                def p3():
                                    c.op("act", lambda e: e.activation(eS[:, 0, :], pss[:], AF.Ln, bias=EPS, scale=1.0 / 128), reads=[b_pss], writes=[b_eS])
                                    c.op("act", lambda e: e.activation(eS[:, 0, :], eS[:, 0, :], AF.Exp, scale=-0.5), reads=[b_eS], writes=[b_eS])
                                    c.op("dve", lambda e: e.scalar_tensor_tensor(yT[:, h, :], eO[:, 0, :], subwc[:, 0:1], eS[:, 0, :], ALU.mult, ALU.mult),
                                         reads=[b_eO, b_eS, b_subwc], writes=[b_yT[h]])
                                def p23():
                                    p2()
                                    p3()
                                return [p1, p23]
                            pending_epi.extend(epilogue())
                    wctx = {}

                    def stepL(s_, qb=qb):
                        def f():
                            t = qb * 4 + s_
                            ht, b_ht = hts.next()
                            c.dma("sp", ht[:], h_dram[t * 128:(t + 1) * 128, :], b_ht, reads=[hd_bufs[t]], writes=[b_ht])
                            wctx[(qb, s_)] = (ht, b_ht)
                        return f

                    def stepM(s_, half, qb=qb):
                        def f():
                            ht, b_ht = wctx[(qb, s_)]
                            pw, b_pw = pwr.next()

                            def mm_wo(e):
                                ins = None
                                for kc in range(NKC):
                                    ins = e.matmul(pw[:], yT[:, kc, s_ * 128:(s_ + 1) * 128], wo[:, kc, half * 512:(half + 1) * 512], start=(kc == 0), stop=(kc == NKC - 1))
                                return ins
                            c.op("pe", mm_wo, reads=b_yT + [b_wo], writes=[b_pw])
                            c.op("dve", lambda e: e.tensor_tensor(ht[:, half * 512:(half + 1) * 512], ht[:, half * 512:(half + 1) * 512], pw[:], ALU.add),
                                 reads=[b_pw, b_ht], writes=[b_ht])
                        return f

                    def stepS(s_, qb=qb):
                        def f():
                            t = qb * 4 + s_
                            ht, b_ht = wctx.pop((qb, s_))
                            emit_ssq("D", ht, b_ht, t, junk, b_junk)
                            c.dma("sp", h_dram[t * 128:(t + 1) * 128, :], ht[:], b_ht, reads=[b_ht], writes=[hd_bufs[t]])
                        return f
                    pending_wo.extend([stepL(0), stepL(1), stepL(2), stepM(0, 0), stepM(0, 1), stepM(1, 0), stepM(1, 1), stepS(0), stepL(3),
                                       stepM(2, 0), stepM(2, 1), stepS(1), stepM(3, 0), stepM(3, 1), stepS(2), stepS(3)])
                while pending_epi:
                    pending_epi.pop(0)()
                while pending_wo:
                    pending_wo.pop(0)()
                c.barrier()

        def phase_G():
            with ExitStack() as st:
                fnw = sb(st, "G_fnw", [128, D], F32)
                b_fnw = c.buf("fnw")
                c.dma("sp", fnw[:], fnw_in.partition_broadcast(128), b_fnw, writes=[b_fnw])
                lnv = sb(st, "G_lnv", [128, NT], F32)
                b_lnv = c.buf("lnv")
                rstd = sb(st, "G_rstd", [128, NT], F32)
                b_rstd = c.buf("rstd")
                sq, b_sq = ssq_glob["E"]
                c.op("act", lambda e: e.activation(lnv[:], sq[:], AF.Ln, bias=EPS, scale=1.0 / D), reads=[b_sq], writes=[b_lnv])
                c.op("act", lambda e: e.activation(rstd[:], lnv[:], AF.Exp, scale=-0.5), reads=[b_lnv], writes=[b_rstd])
                hts = Rot([(sb(st, "G_ht%d" % i, [128, D], F32), c.buf("ht")) for i in range(4)])
                for t in range(NT):
                    ht, b_ht = hts.next()
                    c.dma("sp", ht[:], h_dram[t * 128:(t + 1) * 128, :], b_ht, reads=[hd_bufs[t]], writes=[b_ht])
                    en = ("dve", "pool")[0]
                    c.op(en, lambda e, ht=ht, t=t: e.scalar_tensor_tensor(ht[:], ht[:], rstd[:, t:t + 1], fnw[:], ALU.mult, ALU.mult),
                         reads=[b_ht, b_rstd, b_fnw], writes=[b_ht])
                    c.dma("sp", out_dram[t * 128:(t + 1) * 128, :], ht[:], b_ht, reads=[b_ht])
                c.barrier()

        if "A" in phases:
            phase_A()
        if "B" in phases:
            phase_F(0, "A", "B")
        if "Q" in phases or "K" in phases or "D" in phases:
            with ExitStack() as kvst:
                VE = sb(kvst, "VE", [128, NT, 8, 129], BF16)
                b_VE = c.buf("VE")
                if "Q" in phases:
                    phase_QV(VE, b_VE)
                KT = sb(kvst, "KT", [128, 8, S], BF16)
                b_KT = [c.buf("KT") for _ in range(8)]
                if "K" in phases:
                    phase_K(KT, b_KT)
                if "D" in phases:
                    phase_D(KT, b_KT, VE, b_VE)
                c.barrier()
        if "E" in phases:
            phase_F(1, "D", "E")
            phase_G()
        c.barrier()
        print("build: nins", c.nins, "nwait", c.nwait, "nsem", len(c.owners) + 5)
        if limit is not None:
            print("LAST:", str(c.last_desc)[:600])
    return nc


def make_consts(S):
    NT = S // 128
    ident = np.eye(128, dtype=np.float32)
    j = np.arange(128)[:, None]
    cidx = np.arange(128)[None, :]
    glamask = ((j <= cidx) & ((j // 64) == (cidx // 64))).astype(np.float32)
    trimask = (j <= cidx).astype(np.float32)
    scan = np.ones((128, 512), np.float32)
    scan[:, ::64] = 0.0
    cst = np.concatenate([ident, glamask, trimask, scan], axis=1)
    pos = np.arange(S, dtype=np.float32)
    inv = (10000.0 ** (-np.arange(0, 64, 2, dtype=np.float32) / 64)).astype(np.float32)
    f = pos[:, None] * inv[None, :]
    emb = np.concatenate([f, f], axis=-1)
    cos = np.cos(emb).astype(np.float32)
    sin = np.sin(emb).astype(np.float32)
    ssin = np.concatenate([-sin[:, :32], sin[:, 32:]], axis=1)
    cos_t = cos.reshape(NT, 128, 64).transpose(1, 0, 2).reshape(128, NT * 64)
    ssin_t = ssin.reshape(NT, 128, 64).transpose(1, 0, 2).reshape(128, NT * 64)
    rope = np.ascontiguousarray(np.concatenate([cos_t, ssin_t], axis=1))
    return np.ascontiguousarray(cst), rope


def pk(v):
    return np.ascontiguousarray(np.asarray(v, np.float32).reshape(8, 128).T)


def shared_inputs(inp, S):
    f = lambda a: np.ascontiguousarray(np.asarray(a, dtype=np.float32))
    cst, rope = make_consts(S)
    nw = np.concatenate([pk(inp["attn_norm_w"][0]), pk(inp["ffn_norm_w"][0]), pk(inp["kv_norm_w"]),
                         pk(inp["attn_norm_w"][1]), pk(inp["ffn_norm_w"][1])], axis=1)
    cw = np.asarray(inp["ffn_conv_w"], np.float32)
    cw = cw.reshape(2, 3, 44, 128).transpose(0, 3, 2, 1).reshape(2, 128, 44 * 3)
    cb = np.asarray(inp["ffn_conv_b"], np.float32).reshape(2, 44, 128).transpose(0, 2, 1)
    return {
        "nw": np.ascontiguousarray(nw), "fnw": f(inp["final_norm_w"]),
        "wqkvg": f(inp["gla_w_qkvg"][0]), "wgk1": f(inp["gla_w_gk1"][0]), "wgk2": f(inp["gla_w_gk2"][0]),
        "bgk": np.ascontiguousarray(np.asarray(inp["gla_b_gk"][0], np.float32).reshape(4, 128).T),
        "onw": f(inp["gla_onorm_w"][0]), "gwo": f(inp["gla_w_o"][0]),
        "wkv": f(inp["w_kv"]), "wq": f(inp["diff_w_q"][0]), "lam": f(np.asarray(inp["diff_lambda"][0]).reshape(256)),
        "subw": f(inp["diff_subln_w"][0]), "dwo": f(inp["diff_w_o"][0]),
        "win": f(inp["ffn_w_in"]), "cw": np.ascontiguousarray(cw), "cb": np.ascontiguousarray(cb),
        "wout": f(inp["ffn_w_out"]), "cst": cst, "rope": rope,
    }


_NC_CACHE = {}


def kernel(**inputs):
    x = np.asarray(inputs["x"], dtype=np.float32)
    B, S, _ = x.shape
    if S not in _NC_CACHE:
        _NC_CACHE[S] = build(S)
    nc = _NC_CACHE[S]
    sh = shared_inputs(inputs, S)
    in_maps = []
    for b in range(B):
        m = dict(sh)
        m["x"] = np.ascontiguousarray(x[b])
        in_maps.append(m)
    res = run_bass_kernel_spmd(nc, in_maps, core_ids=list(range(B)))
    return np.stack([np.asarray(r["out"]) for r in res.results], axis=0).astype(np.float32)
```

```python
import numpy as np
from contextlib import ExitStack
import concourse.bass as bass
import concourse.mybir as mybir
from concourse.bass_utils import run_bass_kernel_spmd

F32 = mybir.dt.float32
BF16 = mybir.dt.bfloat16
AF = mybir.ActivationFunctionType
ALU = mybir.AluOpType
AX = mybir.AxisListType

D = 1024
NKC = 8
DFF = 2816
NFC = 22
EPS = 1e-6
LAM_INIT = 0.8 - 0.6 * float(np.exp(-0.3 * 1))


class Buf:
    __slots__ = ("name", "w", "r", "dsem", "dcnt", "excl")

    def __init__(self, name, excl=False):
        self.name = name
        self.excl = excl
        self.w = None
        self.r = {}
        self.dsem = None
        self.dcnt = 0


class Eng:
    def __init__(self, name, e, sem):
        self.name, self.e, self.sem = name, e, sem
        self.cnt = 0
        self.seen = {}


class Ctx:
    def __init__(self, nc, stack):
        self.nc = nc
        self.stack = stack
        self.engs = {}
        for n, a in [("pe", "tensor"), ("act", "scalar"), ("dve", "vector"), ("pool", "gpsimd"), ("sp", "sync")]:
            self.engs[n] = Eng(n, getattr(nc, a), stack.enter_context(nc.semaphore("s_" + n)))
        self.owners = []
        self.nwait = 0
        self.nins = 0
        self.uid = 0
        self.limit = None
        self.last_desc = None

    def buf(self, name, excl=False):
        self.uid += 1
        return Buf("%s_%d" % (name, self.uid), excl)

    def _wait(self, E, toks, rawkeys):
        best = {}
        for t in toks:
            if t is None:
                continue
            k, sem, val, en = t
            if en == E.name and en == "pe":
                continue
            if k not in best or best[k][2] < val:
                best[k] = t
        for k, (_, sem, val, en) in best.items():
            if E.seen.get(k, 0) < val:
                E.e.wait_ge(sem, val)
                E.seen[k] = val
                self.nwait += 1

    def _deps(self, E, reads, writes):
        toks = []
        rawkeys = set()
        for b in reads:
            if b.w is not None:
                toks.append(b.w)
                if b.w[3] == E.name:
                    rawkeys.add(b.w[0])
            if b.excl:
                toks.extend(t for t in b.r.values() if t[3] != E.name)
        for b in writes:
            if b.w is not None:
                toks.append(b.w)
            toks.extend(b.r.values())
        self._wait(E, toks, rawkeys)

    def _commit(self, tok, reads, writes):
        k = tok[0]
        for b in reads:
            if k not in b.r or b.r[k][2] < tok[2]:
                b.r[k] = tok
        for b in writes:
            b.w = tok
            b.r = {}

    def op(self, en, emit, reads=(), writes=()):
        E = self.engs[en]
        if self.limit is not None and self.nins >= self.limit:
            return None
        self._deps(E, reads, writes)
        ins = emit(E.e)
        self.last_desc = ins
        E.cnt += 1
        ins.then_inc(E.sem, 1)
        self.nins += 1
        tok = ("e_" + en, E.sem, E.cnt, en)
        self._commit(tok, reads, writes)
        return tok

    def dma(self, q, out, in_, owner, reads=(), writes=(), **kw):
        E = self.engs[q]
        if self.limit is not None and self.nins >= self.limit:
            return None
        self._deps(E, reads, writes)
        if owner.dsem is None:
            owner.dsem = self.stack.enter_context(self.nc.semaphore("d_" + owner.name))
            self.owners.append(owner)
        ins = E.e.dma_start(out=out, in_=in_, **kw)
        owner.dcnt += 16
        ins.then_inc(owner.dsem, 16)
        self.nins += 1
        tok = ("d_" + owner.name, owner.dsem, owner.dcnt, "dma")
        self._commit(tok, reads, writes)
        return tok

    def barrier(self):
        toks = [("e_" + E.name, E.sem, E.cnt, E.name) for E in self.engs.values() if E.cnt > 0]
        toks += [("d_" + b.name, b.dsem, b.dcnt, "dma") for b in self.owners]
        for E in self.engs.values():
            mine = [t for t in toks if t[3] != E.name]
            self._wait(E, mine, set())


class Rot:
    def __init__(self, items):
        self.items = items
        self.i = 0

    def next(self):
        it = self.items[self.i % len(self.items)]
        self.i += 1
        return it


def build(S=4096, phases="ABQKDE", debug=False, limit=None):
    NT = S // 128
    NB = S // 512
    nc = bass.Bass("TRN2", target_bir_lowering=False)

    def din(name, shape, dt=F32):
        return nc.dram_tensor(name, list(shape), dt, kind="ExternalInput").ap()

    x_in = din("x", [S, D])
    nw_in = din("nw", [128, 40])
    fnw_in = din("fnw", [D])
    wqkvg_in = din("wqkvg", [D, 3072])
    wgk1_in = din("wgk1", [D, 16])
    wgk2_in = din("wgk2", [16, 512])
    bgk_in = din("bgk", [128, 4])
    onw_in = din("onw", [256])
    gwo_in = din("gwo", [D, D])
    wkv_in = din("wkv", [D, 2048])
    wq_in = din("wq", [D, D])
    lam_in = din("lam", [256])
    subw_in = din("subw", [128])
    dwo_in = din("dwo", [D, D])
    win_in = din("win", [2, D, 2 * DFF])
    cw_in = din("cw", [2, 128, 44 * 3])
    cb_in = din("cb", [2, 128, 44])
    wout_in = din("wout", [2, DFF, D])
    cst_in = din("cst", [128, 896])
    rope_in = din("rope", [128, 2 * NT * 64])
    out_dram = nc.dram_tensor("out", [S, D], F32, kind="ExternalOutput").ap()
    scratch_kind = "ExternalOutput" if debug else "Internal"
    h_dram = nc.dram_tensor("hscr", [S, D], F32, kind=scratch_kind).ap()
    qt_dram = nc.dram_tensor("qtscr", [NB * 8 * 128, 1024], BF16, kind=scratch_kind).ap()

    with ExitStack() as gst:
        c = Ctx(nc, gst)
        c.limit = limit
        hd_bufs = [c.buf("hd") for _ in range(NT)]
        qt_bufs = [c.buf("qtd") for _ in range(NB * 8)]

        def sb(st, name, shape, dt):
            c.uid += 1
            return st.enter_context(nc.sbuf_tensor("sb%d_%s" % (c.uid, name), list(shape), dt))

        def ps(st, name, shape, dt):
            c.uid += 1
            return st.enter_context(nc.psum_tensor("ps%d_%s" % (c.uid, name), list(shape), dt))

        cst = sb(gst, "cst", [128, 896], F32)
        b_cst = c.buf("cst")
        identb = sb(gst, "identb", [128, 128], BF16)
        b_identb = c.buf("identb")
        trib = sb(gst, "trib", [128, 128], BF16)
        b_trib = c.buf("trib")
        nw = sb(gst, "nw", [128, 40], F32)
        b_nw = c.buf("nw")
        c.dma("sp", cst[:], cst_in[:, :], b_cst, writes=[b_cst])
        c.dma("sp", nw[:], nw_in[:, :], b_nw, writes=[b_nw])
        c.op("dve", lambda e: e.tensor_copy(identb[:], cst[:, 0:128]), reads=[b_cst], writes=[b_identb])
        c.op("dve", lambda e: e.tensor_copy(trib[:], cst[:, 256:384]), reads=[b_cst], writes=[b_trib])
        negmask = sb(gst, "negmask", [128, 128], BF16)
        b_negmask = c.buf("negmask")
        c.op("dve", lambda e: e.tensor_scalar(negmask[:], cst[:, 256:384], -1.0, 30000.0, ALU.add, ALU.mult), reads=[b_cst], writes=[b_negmask])
        glamask = cst[:, 128:256]
        scanmask = cst[:, 384:896]

        cast_rr = [0]

        def load_w(st, dst, b_dst, src, nk, ncols, stg, scale_col=None, cmul=None, col_lo=0, src_col0=0):
            for kc in range(nk):
                c0 = 0
                while c0 < ncols:
                    w = min(2048, ncols - c0)
                    sap, sbuf_ = stg.next()
                    c.dma("sp", sap[:, 0:w], src[kc * 128:(kc + 1) * 128, src_col0 + c0:src_col0 + c0 + w], sbuf_, writes=[sbuf_])
                    use_act = (cast_rr[0] % 2 == 1) and cmul is None
                    cast_rr[0] += 1
                    o = dst[:, kc, col_lo + c0:col_lo + c0 + w]
                    if scale_col is None and cmul is None:
                        if use_act:
                            c.op("act", lambda e, o=o, sap=sap, w=w: e.copy(o, sap[:, 0:w]), reads=[sbuf_], writes=[b_dst])
                        else:
                            c.op("dve", lambda e, o=o, sap=sap, w=w: e.tensor_copy(o, sap[:, 0:w]), reads=[sbuf_], writes=[b_dst])
                    elif scale_col is None:
                        c.op("dve", lambda e, o=o, sap=sap, w=w: e.tensor_scalar(o, sap[:, 0:w], float(cmul), None, ALU.mult),
                             reads=[sbuf_], writes=[b_dst])
                    else:
                        sc = nw[:, scale_col + kc:scale_col + kc + 1]
                        if cmul is None:
                            if use_act:
                                c.op("act", lambda e, o=o, sap=sap, w=w, sc=sc: e.activation(o, sap[:, 0:w], AF.Copy, scale=sc),
                                     reads=[sbuf_, b_nw], writes=[b_dst])
                            else:
                                c.op("dve", lambda e, o=o, sap=sap, w=w, sc=sc: e.tensor_scalar(o, sap[:, 0:w], sc, None, ALU.mult),
                                     reads=[sbuf_, b_nw], writes=[b_dst])
                        else:
                            c.op("dve", lambda e, o=o, sap=sap, w=w, sc=sc: e.tensor_scalar(o, sap[:, 0:w], sc, float(cmul), ALU.mult, ALU.mult),
                                 reads=[sbuf_, b_nw], writes=[b_dst])
                    c0 += w

        def mk_stage(st, n=3):
            items = []
            for i in range(n):
                items.append((sb(st, "stg%d" % i, [128, 2048], F32), c.buf("stg")))
            return Rot(items)

        ssq_glob = {}
        for nm in ("A", "B", "D", "E"):
            ssq_glob[nm] = (sb(gst, "ssq" + nm, [128, NT], F32), c.buf("ssq" + nm))

        def make_norm(st, pre=None, nhnb=2, nht=3):
            hts = Rot([(sb(st, "ht%d" % i, [128, D], F32), c.buf("ht")) for i in range(nht)])
            hnbs = Rot([(sb(st, "hnb%d" % i, [128, D], BF16), c.buf("hnb")) for i in range(nhnb)])
            ncol = NT if pre is not None else 4
            ssq = sb(st, "ssq", [128, 4], F32)
            b_ssq = c.buf("ssq")
            lnv = sb(st, "lnv", [128, ncol], F32)
            b_lnv = c.buf("lnv")
            rstd = sb(st, "rstd", [128, ncol], F32)
            b_rstd = c.buf("rstd")
            if pre is not None:
                sq, b_sq = ssq_glob[pre]
                c.op("act", lambda e: e.activation(lnv[:], sq[:], AF.Ln, bias=EPS, scale=1.0 / D), reads=[b_sq], writes=[b_lnv])
                c.op("act", lambda e: e.activation(rstd[:], lnv[:], AF.Exp, scale=-0.5), reads=[b_lnv], writes=[b_rstd])

            def norm_block(src, src_bufs, blk, hnT, b_hnT, ptr_rot):
                if pre is None:
                    for j in range(4):
                        t = blk * 4 + j
                        ht, b_ht = hts.next()
                        hnb, b_hnb = hnbs.next()
                        c.dma("sp", ht[:], src[t * 128:(t + 1) * 128, :], b_ht, reads=[src_bufs[t]], writes=[b_ht])
                        c.op("act", lambda e, ht=ht, hnb=hnb, j=j: e.activation(hnb[:], ht[:], AF.Square, accum_out=ssq[:, j:j + 1]),
                             reads=[b_ht], writes=[b_hnb, b_ssq])
                    c.op("act", lambda e: e.activation(lnv[:], ssq[:], AF.Ln, bias=EPS, scale=1.0 / D), reads=[b_ssq], writes=[b_lnv])
                    c.op("act", lambda e: e.activation(rstd[:], lnv[:], AF.Exp, scale=-0.5), reads=[b_lnv], writes=[b_rstd])
                for j in range(4):
                    stage1_tile(src, src_bufs, blk, j)
                    stage2_tile(hnT, b_hnT, ptr_rot, j)

            pend = {}

            def stage1_tile(src, src_bufs, blk, j):
                t = blk * 4 + j
                col = t if pre is not None else j
                ht, b_ht = hts.next()
                hnb, b_hnb = hnbs.next()
                c.dma("sp", ht[:], src[t * 128:(t + 1) * 128, :], b_ht, reads=[src_bufs[t]], writes=[b_ht])
                c.op("dve", lambda e: e.tensor_scalar(hnb[:], ht[:], rstd[:, col:col + 1], None, ALU.mult),
                     reads=[b_ht, b_rstd], writes=[b_hnb])
                pend[j] = (hnb, b_hnb)

            def stage2_tile(hnT, b_hnT, ptr_rot, j):
                hnb, b_hnb = pend.pop(j)
                ptr, b_ptr = ptr_rot.next()

                def tr(e):
                    ins = None
                    for kc in range(NKC):
                        ins = e.transpose(ptr[:, kc, :], hnb[:, kc * 128:(kc + 1) * 128], identb[:])
                    return ins
                c.op("pe", tr, reads=[b_hnb, b_identb], writes=[b_ptr])
                c.op("act", lambda e: e.copy(hnT[:, :, j * 128:(j + 1) * 128], ptr[:]), reads=[b_ptr], writes=[b_hnT])

            def stage1(src, src_bufs, blk):
                assert pre is not None and nhnb >= 4
                for j in range(4):
                    stage1_tile(src, src_bufs, blk, j)

            def stage2(hnT, b_hnT, ptr_rot):
                for j in range(4):
                    stage2_tile(hnT, b_hnT, ptr_rot, j)
            norm_block.stage1 = stage1
            norm_block.stage2 = stage2
            return norm_block, hts

        def emit_ssq(name, ht, b_ht, t, junk, b_junk):
            sq, b_sq = ssq_glob[name]
            c.op("act", lambda e: e.activation(junk[:], ht[:], AF.Square, accum_out=sq[:, t:t + 1]), reads=[b_ht], writes=[b_junk, b_sq])

        def phase_A():
            with ExitStack() as st:
                wq = sb(st, "A_wqkvg", [128, NKC, 3072], BF16)
                b_wq = c.buf("A_wqkvg")
                wg1 = sb(st, "A_wg1", [128, NKC, 16], BF16)
                b_wg1 = c.buf("A_wg1")
                wg2 = sb(st, "A_wg2", [16, 512], BF16)
                b_wg2 = c.buf("A_wg2")
                wo = sb(st, "A_wo", [128, NKC, D], BF16)
                b_wo = c.buf("A_wo")
                negb = sb(st, "A_negb", [128, 4], F32)
                b_negb = c.buf("A_negb")
                onw = sb(st, "A_onw", [128, 256], F32)
                b_onw = c.buf("A_onw")
                with ExitStack() as st2:
                    stg = mk_stage(st2)
                    load_w(st2, wq, b_wq, wqkvg_in, NKC, 512, stg, scale_col=0, cmul=128 ** -0.5, col_lo=0, src_col0=0)
                    load_w(st2, wq, b_wq, wqkvg_in, NKC, 2560, stg, scale_col=0, col_lo=512, src_col0=512)
                    load_w(st2, wg1, b_wg1, wgk1_in, NKC, 16, stg, scale_col=0)
                    load_w(st2, wo, b_wo, gwo_in, NKC, D, stg)
                    sap, sbf = stg.next()
                    c.dma("sp", sap[0:16, 0:512], wgk2_in[:, :], sbf, writes=[sbf])
                    c.op("dve", lambda e: e.tensor_copy(wg2[:], sap[0:16, 0:512]), reads=[sbf], writes=[b_wg2])
                    sap2, sbf2 = stg.next()
                    c.dma("sp", sap2[:, 0:4], bgk_in[:, :], sbf2, writes=[sbf2])
                    c.op("dve", lambda e: e.tensor_scalar(negb[:], sap2[:, 0:4], -1.0, None, ALU.mult), reads=[sbf2], writes=[b_negb])
                    c.dma("sp", onw[:], onw_in.partition_broadcast(128), b_onw, writes=[b_onw])
                    c.barrier()
                norm_block, hts = make_norm(st)
                hnT = sb(st, "A_hnT", [128, NKC, 512], BF16)
                b_hnT = c.buf("A_hnT")
                g1s = sb(st, "A_g1s", [16, 512], BF16)
                b_g1s = c.buf("A_g1s")
                e1 = Rot([(sb(st, "A_e1_%d" % i, [128, 512], F32), c.buf("e1")) for i in range(2)])
                cc = Rot([(sb(st, "A_cc_%d" % i, [128, 512], F32), c.buf("cc")) for i in range(2)])
                ebs = Rot([(sb(st, "A_eb_%d" % i, [128, 512], F32), c.buf("eb")) for i in range(2)])
                enbs = Rot([(sb(st, "A_enb_%d" % i, [128, 512], F32), c.buf("enb")) for i in range(2)])
                dds = Rot([(sb(st, "A_dd_%d" % i, [128, 512], F32), c.buf("dd")) for i in range(2)])
                qin = [sb(st, "A_qin%d" % h, [128, 512], BF16) for h in range(4)]
                kin = [sb(st, "A_kin%d" % h, [128, 512], BF16) for h in range(4)]
                qp0 = [sb(st, "A_qp0_%d" % h, [128, 4, 128], BF16) for h in range(4)]
                qp1 = [sb(st, "A_qp1_%d" % h, [128, 4, 128], BF16) for h in range(4)]
                kea = [sb(st, "A_kea%d" % h, [128, 4, 128], BF16) for h in range(4)]
                keb = [sb(st, "A_keb%d" % h, [128, 4, 128], BF16) for h in range(4)]
                dec = [sb(st, "A_dec%d" % h, [128, 8], F32) for h in range(4)]
                b_qin = [c.buf("qin") for _ in range(4)]
                b_kin = [c.buf("kin") for _ in range(4)]
                b_qp0 = [c.buf("qp0") for _ in range(4)]
                b_qp1 = [c.buf("qp1") for _ in range(4)]
                b_kea = [c.buf("kea") for _ in range(4)]
                b_keb = [c.buf("keb") for _ in range(4)]
                b_dec = [c.buf("dec") for _ in range(4)]
                for h in range(4):
                    for tl, bb in ((qp0[h], b_qp0[h]), (qp1[h], b_qp1[h]), (kea[h], b_kea[h]), (keb[h], b_keb[h])):
                        c.op("pool", lambda e, tl=tl: e.memset(tl[:], 0.0), writes=[bb])
                vts = Rot([(sb(st, "A_v%d" % i, [128, D], BF16), c.buf("v")) for i in range(4)])
                gws = Rot([(sb(st, "A_gw%d" % i, [128, D], F32), c.buf("gw")) for i in range(4)])
                osb = Rot([(sb(st, "A_osb%d" % i, [128, D], F32), c.buf("osb")) for i in range(4)])
                ket = Rot([(sb(st, "A_ket%d" % i, [128, 2, 128], BF16), c.buf("ket")) for i in range(3)])
                stm = Rot([(sb(st, "A_stm%d" % i, [128, 128], BF16), c.buf("stm")) for i in range(3)])
                ys = Rot([(sb(st, "A_y%d" % i, [128, D], BF16), c.buf("y")) for i in range(2)])
                yTs = Rot([(sb(st, "A_yT%d" % i, [128, NKC, 128], BF16), c.buf("yT")) for i in range(2)])
                ssqo = sb(st, "A_ssqo", [128, 16], F32)
                b_ssqo = c.buf("ssqo")
                lno = sb(st, "A_lno", [128, 16], F32)
                b_lno = c.buf("lno")
                rso = sb(st, "A_rso", [128, 16], F32)
                b_rso = c.buf("rso")
                junk = sb(st, "A_junk", [128, 256], BF16)
                b_junk = c.buf("junk")
                junk1k = sb(st, "A_junk1k", [128, D], BF16)
                b_junk1k = c.buf("junk1k")
                Sf = [[sb(st, "A_S%d_%d" % (h, i), [128, 256], F32) for i in range(2)] for h in range(4)]
                Sb = [[sb(st, "A_Sb%d_%d" % (h, i), [128, 256], BF16) for i in range(2)] for h in range(4)]
                b_Sf = [[c.buf("Sf") for i in range(2)] for h in range(4)]
                b_Sb = [[c.buf("Sb") for i in range(2)] for h in range(4)]
                for h in range(4):
                    c.op("pool", lambda e, h=h: e.memset(Sf[h][0][:], 0.0), writes=[b_Sf[h][0]])
                    c.op("pool", lambda e, h=h: e.memset(Sb[h][0][:], 0.0), writes=[b_Sb[h][0]])
                mm = Rot([(ps(st, "A_mm%d" % i, [128, 512], F32), c.buf("mm", True)) for i in range(3)])
                ptr_rot = Rot([(ps(st, "A_ptr%d" % i, [128, NKC, 128], BF16), c.buf("ptr", True)) for i in range(1)])
                pst = ps(st, "A_pst", [128, 512], F32)
                b_pstb = c.buf("pst", True)
                pst_rot = Rot([(pst[:, i * 128:(i + 1) * 128], b_pstb) for i in range(4)])
                pco_banks = [(ps(st, "A_pco%d" % i, [128, 512], F32), c.buf("pco", True)) for i in range(2)]
                po = ps(st, "A_po", [128, 512], F32)
                b_pob = c.buf("po", True)
                po_rot = Rot([(po[:, i * 256:(i + 1) * 256], b_pob) for i in range(2)])

                x_bufs = [Buf("xin")] * NT
                norm_block(x_in, x_bufs, 0, hnT, b_hnT, ptr_rot)
                for blk in range(NB):
                    pg, b_pg = mm.next()

                    def mm_g1(e, pg=pg):
                        ins = None
                        for kc in range(NKC):
                            ins = e.matmul(pg[0:16, :], wg1[:, kc, :], hnT[:, kc, :], start=(kc == 0), stop=(kc == NKC - 1))
                        return ins
                    c.op("pe", mm_g1, reads=[b_wg1, b_hnT], writes=[b_pg])
                    c.op("act", lambda e, pg=pg: e.copy(g1s[:], pg[0:16, :]), reads=[b_pg], writes=[b_g1s])
                    for h in range(4):
                        pgk, b_pgk = mm.next()
                        c.op("pe", lambda e, pgk=pgk, h=h: e.matmul(pgk[:], wg2[:, h * 128:(h + 1) * 128], g1s[:], start=True, stop=True),
                             reads=[b_wg2, b_g1s], writes=[b_pgk])
                        e1t, b_e1 = e1.next()
                        cct, b_cc = cc.next()
                        ebt, b_eb = ebs.next()
                        enbt, b_enb = enbs.next()
                        ddt, b_dd = dds.next()
                        c.op("act", lambda e, e1t=e1t, pgk=pgk, h=h: e.activation(e1t[:], pgk[:], AF.Exp, bias=negb[:, h:h + 1], scale=-1.0),
                             reads=[b_pgk, b_negb], writes=[b_e1])
                        c.op("act", lambda e, e1t=e1t: e.activation(e1t[:], e1t[:], AF.Ln, bias=1.0, scale=1.0), reads=[b_e1], writes=[b_e1])
                        c.op("dve", lambda e, cct=cct, e1t=e1t: e.tensor_tensor_scan(cct[:], scanmask, e1t[:], 0.0, ALU.mult, ALU.add),
                             reads=[b_e1, b_cst], writes=[b_cc])
                        c.op("act", lambda e, ebt=ebt, cct=cct: e.activation(ebt[:], cct[:], AF.Exp, scale=-1.0 / 16), reads=[b_cc], writes=[b_eb])
                        c.op("act", lambda e, enbt=enbt, cct=cct: e.activation(enbt[:], cct[:], AF.Exp, scale=1.0 / 16), reads=[b_cc], writes=[b_enb])
                        cc3 = cct[:].rearrange("p (n c) -> p n c", c=64)
                        c.op("pool", lambda e, ddt=ddt, cc3=cc3: e.tensor_tensor(ddt[:].rearrange("p (n c) -> p n c", c=64),
                                                                                  cc3[:, :, 63:64].to_broadcast([128, 8, 64]), cc3, ALU.subtract),
                             reads=[b_cc], writes=[b_dd])
                        c.op("act", lambda e, ddt=ddt: e.activation(ddt[:], ddt[:], AF.Exp, scale=-1.0 / 16), reads=[b_dd], writes=[b_dd])
                        eb3 = ebt[:].rearrange("p (n c) -> p n c", c=64)
                        c.op("pool", lambda e, h=h, eb3=eb3: e.tensor_copy(dec[h][:].unsqueeze(2), eb3[:, :, 63:64]), reads=[b_eb], writes=[b_dec[h]])
                        pq, b_pq = mm.next()

                        def mm_q(e, pq=pq, h=h):
                            ins = None
                            for kc in range(NKC):
                                ins = e.matmul(pq[:], wq[:, kc, h * 128:(h + 1) * 128], hnT[:, kc, :], start=(kc == 0), stop=(kc == NKC - 1))
                            return ins
                        c.op("pe", mm_q, reads=[b_wq, b_hnT], writes=[b_pq])
                        c.op("dve", lambda e, h=h, pq=pq, ebt=ebt: e.tensor_tensor(qin[h][:], pq[:], ebt[:], ALU.mult),
                             reads=[b_pq, b_eb], writes=[b_qin[h]])
                        q4 = qin[h][:].rearrange("p (t c) -> p t c", c=128)
                        c.op("pool", lambda e, h=h, q4=q4: e.tensor_copy(qp0[h][:, :, 0:64], q4[:, :, 0:64]), reads=[b_qin[h]], writes=[b_qp0[h]])
                        c.op("pool", lambda e, h=h, q4=q4: e.tensor_copy(qp1[h][:, :, 64:128], q4[:, :, 64:128]), reads=[b_qin[h]], writes=[b_qp1[h]])
                        pk, b_pk = mm.next()

                        def mm_k(e, pk=pk, h=h):
                            ins = None
                            for kc in range(NKC):
                                ins = e.matmul(pk[:], wq[:, kc, 512 + h * 128:512 + (h + 1) * 128], hnT[:, kc, :], start=(kc == 0), stop=(kc == NKC - 1))
                            return ins
                        c.op("pe", mm_k, reads=[b_wq, b_hnT], writes=[b_pk])
                        c.op("dve", lambda e, h=h, pk=pk, enbt=enbt: e.tensor_tensor(kin[h][:], pk[:], enbt[:], ALU.mult),
                             reads=[b_pk, b_enb], writes=[b_kin[h]])
                        pk4 = pk[:].rearrange("p (t c) -> p t c", c=128)
                        dd4 = ddt[:].rearrange("p (t c) -> p t c", c=128)
                        c.op("dve", lambda e, h=h, pk4=pk4, dd4=dd4: e.tensor_tensor(kea[h][:, :, 0:64], pk4[:, :, 0:64], dd4[:, :, 0:64], ALU.mult),
                             reads=[b_pk, b_dd], writes=[b_kea[h]])
                        c.op("dve", lambda e, h=h, pk4=pk4, dd4=dd4: e.tensor_tensor(keb[h][:, :, 64:128], pk4[:, :, 64:128], dd4[:, :, 64:128], ALU.mult),
                             reads=[b_pk, b_dd], writes=[b_keb[h]])
                    tiles = []
                    for j in range(4):
                        vt, b_vt = vts.next()
                        gw, b_gw = gws.next()
                        for half in range(2):
                            pv, b_pv = mm.next()

                            def mm_v(e, pv=pv, j=j, half=half):
                                ins = None
                                for kc in range(NKC):
                                    ins = e.matmul(pv[:], hnT[:, kc, j * 128:(j + 1) * 128], wq[:, kc, 1024 + half * 512:1024 + (half + 1) * 512],
                                                   start=(kc == 0), stop=(kc == NKC - 1))
                                return ins
                            c.op("pe", mm_v, reads=[b_wq, b_hnT], writes=[b_pv])
                            c.op("dve", lambda e, vt=vt, pv=pv, half=half: e.tensor_copy(vt[:, half * 512:(half + 1) * 512], pv[:]),
                                 reads=[b_pv], writes=[b_vt])
                        tiles.append((vt, b_vt, gw, b_gw))
                    for j in range(4):
                        vt, b_vt, gw, b_gw = tiles[j]
                        for half in range(2):
                            pgm, b_pgm = mm.next()

                            def mm_g(e, pgm=pgm, j=j, half=half):
                                ins = None
                                for kc in range(NKC):
                                    ins = e.matmul(pgm[:], hnT[:, kc, j * 128:(j + 1) * 128], wq[:, kc, 2048 + half * 512:2048 + (half + 1) * 512],
                                                   start=(kc == 0), stop=(kc == NKC - 1))
                                return ins
                            c.op("pe", mm_g, reads=[b_wq, b_hnT], writes=[b_pgm])
                            c.op("act", lambda e, gw=gw, pgm=pgm, half=half: e.activation(gw[:, half * 512:(half + 1) * 512], pgm[:], AF.Silu),
                                 reads=[b_pgm], writes=[b_gw])
                        c.op("dve", lambda e, gw=gw: e.tensor_tensor(gw[:].rearrange("p (h e) -> p h e", h=4), gw[:].rearrange("p (h e) -> p h e", h=4),
                                                                     onw[:].unsqueeze(1).to_broadcast([128, 4, 256]), ALU.mult),
                             reads=[b_gw, b_onw], writes=[b_gw])
                    if blk + 1 < NB:
                        norm_block(x_in, x_bufs, blk + 1, hnT, b_hnT, ptr_rot)
                    osbs = []
                    for j in range(4):
                        ot, b_ot = osb.next()
                        osbs.append((ot, b_ot))

                    def gla_front(i):
                        j, h = divmod(i, 4)
                        vt, b_vt, gw, b_gw = tiles[j]
                        cs = slice(j * 128, (j + 1) * 128)
                        pstt, b_pstt = pst_rot.next()
                        c.op("pe", lambda e: e.matmul(pstt, kin[h][:, cs], qin[h][:, cs], start=True, stop=True),
                             reads=[b_kin[h], b_qin[h]], writes=[b_pstt])
                        smt, b_smt = stm.next()
                        c.op("dve", lambda e: e.tensor_tensor(smt[:], pstt, glamask, ALU.mult),
                             reads=[b_pstt, b_cst], writes=[b_smt])
                        ptr, b_ptr = ptr_rot.next()

                        def tr_k(e):
                            e.transpose(ptr[:, 0, :], kea[h][:, j, :], identb[:])
                            return e.transpose(ptr[:, 1, :], keb[h][:, j, :], identb[:])
                        c.op("pe", tr_k, reads=[b_kea[h], b_keb[h], b_identb], writes=[b_ptr])
                        kt_, b_kt = ket.next()
                        c.op("act", lambda e: e.copy(kt_[:], ptr[:, 0:2, :]), reads=[b_ptr], writes=[b_kt])
                        vh = vt[:, h * 256:(h + 1) * 256]
                        pcb, b_pcb = pco_banks[i % 2]
                        pc0 = pcb[:, 0:256]
                        pc1 = pcb[:, 256:512]
                        c.op("pe", lambda e: e.matmul(pc0, kt_[:, 0, :], vh, start=True, stop=True), reads=[b_kt, b_vt], writes=[b_pcb])
                        c.op("pe", lambda e: e.matmul(pc1, kt_[:, 1, :], vh, start=True, stop=True), reads=[b_kt, b_vt], writes=[b_pcb])
                        return (smt, b_smt, vh, b_vt, pc0, pc1, b_pcb)

                    def gla_back(i, fr):
                        j, h = divmod(i, 4)
                        smt, b_smt, vh, b_vt, pc0, pc1, b_pcb = fr
                        ot, b_ot = osbs[j]
                        n0 = 2 * j
                        c.op("dve", lambda e: e.scalar_tensor_tensor(Sf[h][1][:], Sf[h][0][:], dec[h][:, n0:n0 + 1], pc0, ALU.mult, ALU.add),
                             reads=[b_Sf[h][0], b_dec[h], b_pcb], writes=[b_Sf[h][1]])
                        c.op("act", lambda e: e.copy(Sb[h][1][:], Sf[h][1][:]), reads=[b_Sf[h][1]], writes=[b_Sb[h][1]])
                        pot, b_pot = po_rot.next()

                        def mm_o(e):
                            e.matmul(pot, smt[:], vh, start=True, stop=False)
                            e.matmul(pot, qp0[h][:, j, :], Sb[h][0][:], start=False, stop=False)
                            return e.matmul(pot, qp1[h][:, j, :], Sb[h][1][:], start=False, stop=True)
                        c.op("pe", mm_o, reads=[b_smt, b_vt, b_qp0[h], b_qp1[h], b_Sb[h][0], b_Sb[h][1]], writes=[b_pot])
                        c.op("dve", lambda e: e.scalar_tensor_tensor(Sf[h][0][:], Sf[h][1][:], dec[h][:, n0 + 1:n0 + 2], pc1, ALU.mult, ALU.add),
                             reads=[b_Sf[h][1], b_dec[h], b_pcb], writes=[b_Sf[h][0]])
                        c.op("act", lambda e: e.copy(Sb[h][0][:], Sf[h][0][:]), reads=[b_Sf[h][0]], writes=[b_Sb[h][0]])
                        c.op("dve", lambda e: e.tensor_copy(ot[:, h * 256:(h + 1) * 256], pot), reads=[b_pot], writes=[b_ot])
                        if h == 3:
                            sqt, b_sqt = junk1k, b_junk1k
                            c.op("pool", lambda e: e.tensor_tensor(sqt[:], ot[:], ot[:], ALU.mult), reads=[b_ot], writes=[b_sqt])
                            deferred_red.append((i + 3, lambda: c.op("dve", lambda e: e.tensor_reduce(ssqo[:, j * 4:(j + 1) * 4], sqt[:].rearrange("p (h e) -> p h e", h=4), AX.X, ALU.add),
                                                                     reads=[b_sqt], writes=[b_ssqo])))
                    deferred_red = []

                    def wo_front(j):
                        t = blk * 4 + j
                        c.op("act", lambda e: e.activation(lno[:, j * 4:(j + 1) * 4], ssqo[:, j * 4:(j + 1) * 4], AF.Ln, bias=EPS, scale=1.0 / 256),
                             reads=[b_ssqo], writes=[b_lno])
                        c.op("act", lambda e: e.activation(rso[:, j * 4:(j + 1) * 4], lno[:, j * 4:(j + 1) * 4], AF.Exp, scale=-0.5), reads=[b_lno], writes=[b_rso])
                        vt, b_vt, gw, b_gw = tiles[j]
                        ot, b_ot = osbs[j]
                        yt, b_yt = ys.next()
                        for h in range(4):
                            col = j * 4 + h
                            hs = slice(h * 256, (h + 1) * 256)
                            c.op("dve", lambda e, hs=hs, col=col: e.scalar_tensor_tensor(yt[:, hs], ot[:, hs], rso[:, col:col + 1], gw[:, hs], ALU.mult, ALU.mult),
                                 reads=[b_ot, b_rso, b_gw], writes=[b_yt])
                        ptr, b_ptr = ptr_rot.next()

                        def tr_y(e):
                            ins = None
                            for kc in range(NKC):
                                ins = e.transpose(ptr[:, kc, :], yt[:, kc * 128:(kc + 1) * 128], identb[:])
                            return ins
                        c.op("pe", tr_y, reads=[b_yt, b_identb], writes=[b_ptr])
                        yT, b_yT = yTs.next()
                        c.op("act", lambda e: e.copy(yT[:], ptr[:]), reads=[b_ptr], writes=[b_yT])
                        ht, b_ht = hts.next()
                        c.dma("sp", ht[:], x_in[t * 128:(t + 1) * 128, :], b_ht, writes=[b_ht])
                        return (t, yT, b_yT, ht, b_ht, yt, b_yt)

                    def wo_back(fr):
                        t, yT, b_yT, ht, b_ht, yt, b_yt = fr
                        for half in range(2):
                            pw, b_pw = mm.next()

                            def mm_wo(e, pw=pw, half=half):
                                ins = None
                                for kc in range(NKC):
                                    ins = e.matmul(pw[:], yT[:, kc, :], wo[:, kc, half * 512:(half + 1) * 512], start=(kc == 0), stop=(kc == NKC - 1))
                                return ins
                            c.op("pe", mm_wo, reads=[b_yT, b_wo], writes=[b_pw])
                            c.op("dve", lambda e, pw=pw, half=half: e.tensor_tensor(ht[:, half * 512:(half + 1) * 512], ht[:, half * 512:(half + 1) * 512], pw[:], ALU.add),
                                 reads=[b_pw, b_ht], writes=[b_ht])
                        emit_ssq("A", ht, b_ht, t, yt, b_yt)
                        c.dma("sp", h_dram[t * 128:(t + 1) * 128, :], ht[:], b_ht, reads=[b_ht], writes=[hd_bufs[t]])
                    sched = {}
                    for j in range(4):
                        sched.setdefault(4 * j + 7, []).append(("f", j))
                        sched.setdefault(4 * j + 9, []).append(("b", j))
                    wfr = {}

                    def run_ev(ev):
                        kind, j = ev
                        if kind == "f":
                            wfr[j] = wo_front(j)
                        else:
                            wo_back(wfr.pop(j))
                    cur_f = gla_front(0)
                    for i in range(16):
                        nx_f = gla_front(i + 1) if i + 1 < 16 else None
                        gla_back(i, cur_f)
                        cur_f = nx_f
                        while deferred_red and deferred_red[0][0] <= i:
                            deferred_red.pop(0)[1]()
                        for ev in sched.pop(i, []):
                            run_ev(ev)
                    while deferred_red:
                        deferred_red.pop(0)[1]()
                    for k in sorted(sched):
                        for ev in sched[k]:
                            run_ev(ev)
                c.barrier()


        def phase_F(l, ssq_in, ssq_out):
            with ExitStack() as st:
                win = sb(st, "F_win", [128, NKC, 2 * DFF], BF16)
                b_wing = [c.buf("F_win") for _ in range(22)]
                wout = sb(st, "F_wout", [128, NFC, D], BF16)
                b_woutc = [c.buf("F_wout") for _ in range(NFC)]
                cw = sb(st, "F_cw", [128, 132], F32)
                b_cw = c.buf("F_cw")
                cb = sb(st, "F_cb", [128, 44], F32)
                b_cb = c.buf("F_cb")
                c.dma("sp", cw[:], cw_in[l, :, :], b_cw, writes=[b_cw])
                c.dma("sp", cb[:], cb_in[l, :, :], b_cb, writes=[b_cb])
                sc_col = 8 if l == 0 else 32
                wflat32 = wout[:].rearrange("p f c -> p (f c)").bitcast(F32)
                NSTG = 3
                STG0 = 10
                stg_bufs = [c.buf("wstg") for _ in range(NSTG)]
                stg_n = [0]
                win_src = win_in[l].rearrange("(k p) c -> p k c", p=128)
                wgroups = []
                for m in range(11):
                    wgroups += [m, 11 + m]
                wstate = {"g": 0, "o": 0}

                def load_win_group():
                    if wstate["g"] >= 22:
                        return
                    g = wgroups[wstate["g"]]
                    wstate["g"] += 1
                    i = stg_n[0] % NSTG
                    stg_n[0] += 1
                    sv = wflat32[:, STG0 * 512 + i * 2048:STG0 * 512 + (i + 1) * 2048].rearrange("p (k c) -> p k c", k=8)
                    c.dma("sp", sv, win_src[:, :, g * 256:(g + 1) * 256], stg_bufs[i], writes=[stg_bufs[i]])
                    if wstate["g"] % 2 == 0:
                        def cast_act(e):
                            ins = None
                            for kc in range(NKC):
                                ins = e.activation(win[:, kc, g * 256:(g + 1) * 256], sv[:, kc, :], AF.Copy, scale=nw[:, sc_col + kc:sc_col + kc + 1])
                            return ins
                        c.op("act", cast_act, reads=[stg_bufs[i], b_nw], writes=[b_wing[g]])
                    else:
                        c.op("dve", lambda e: e.tensor_tensor(win[:, :, g * 256:(g + 1) * 256], sv,
                                                              nw[:, sc_col:sc_col + 8].unsqueeze(2).to_broadcast([128, 8, 256]), ALU.mult),
                             reads=[stg_bufs[i], b_nw], writes=[b_wing[g]])

                def load_wout_chunk():
                    fc = wstate["o"]
                    if fc >= NFC:
                        return
                    if fc >= STG0 and wstate["g"] < 22:
                        return
                    wstate["o"] += 1
                    ht, b_ht = hts.next()
                    c.dma("sp", ht[:], wout_in[l][fc * 128:(fc + 1) * 128, :], b_ht, writes=[b_ht])
                    wr = [b_woutc[fc]]
                    if fc >= STG0:
                        wr.append(stg_bufs[(fc - STG0) // 4])
                    c.op("act", lambda e: e.copy(wout[:, fc, :], ht[:]), reads=[b_ht], writes=wr)
                norm_block, hts = make_norm(st, pre=ssq_in, nhnb=4, nht=2)
                hnT = sb(st, "F_hnT", [128, NKC, 512], BF16)
                b_hnT = c.buf("F_hnT")
                actT = sb(st, "F_actT", [128, NFC, 512], BF16)
                b_actT = [c.buf("actT") for _ in range(NFC)]
                Us = {r: Rot([(sb(st, "F_U%s%d" % (r, i), [128, 514], F32), c.buf("U"), c.buf("Uh")) for i in range(2)]) for r in "ag"}
                Xs = {r: Rot([(sb(st, "F_X%s%d" % (r, i), [128, 512], F32), c.buf("X")) for i in range(3)]) for r in "ag"}
                halo = sb(st, "F_halo", [128, 44, 2], F32)
                b_halo = [c.buf("halo") for _ in range(44)]
                c.op("pool", lambda e: e.memset(halo[:], 0.0), writes=b_halo)
                junk = sb(st, "F_junk", [128, D], BF16)
                b_junk = c.buf("junk")
                mm = Rot([(ps(st, "F_mm%d" % i, [128, 512], F32), c.buf("mm", True)) for i in range(4)])
                ptr_rot = Rot([(ps(st, "F_ptr%d" % i, [128, NKC, 128], BF16), c.buf("ptr", True)) for i in range(1)])
                wo_rot = Rot([(ps(st, "F_wo%d" % i, [128, 512], F32), c.buf("wo", True)) for i in range(3)])
                norm_block(h_dram, hd_bufs, 0, hnT, b_hnT, ptr_rot)
                for _ in range(4):
                    load_win_group()
                for blk in range(NB):
                    def ffn_front(cp):
                        xr = {}
                        for role, ch in (("a", cp), ("g", NFC + cp)):
                            pm, b_pm = mm.next()

                            def mm_in(e, pm=pm, ch=ch):
                                ins = None
                                for kc in range(NKC):
                                    ins = e.matmul(pm[:], win[:, kc, ch * 128:(ch + 1) * 128], hnT[:, kc, :], start=(kc == 0), stop=(kc == NKC - 1))
                                return ins
                            c.op("pe", mm_in, reads=[b_wing[ch // 2], b_hnT], writes=[b_pm])
                            U, b_U, b_Uh = Us[role].next()
                            X, b_X = Xs[role].next()
                            c.op("pool", lambda e, U=U, ch=ch: e.tensor_copy(U[:, 0:2], halo[:, ch, :]), reads=[b_halo[ch]], writes=[b_Uh])
                            c.op("act", lambda e, U=U, pm=pm: e.copy(U[:, 2:514], pm[:]), reads=[b_pm], writes=[b_U])
                            c.op("act", lambda e, X=X, pm=pm, ch=ch: e.activation(X[:], pm[:], AF.Identity, bias=cb[:, ch:ch + 1], scale=cw[:, ch * 3 + 2:ch * 3 + 3]),
                                 reads=[b_pm, b_cw, b_cb], writes=[b_X])
                            c.op("pool", lambda e, U=U, ch=ch: e.tensor_copy(halo[:, ch, :], U[:, 512:514]), reads=[b_U], writes=[b_halo[ch]])
                            c.op("dve", lambda e, X=X, U=U, ch=ch: e.scalar_tensor_tensor(X[:], U[:, 1:513], cw[:, ch * 3 + 1:ch * 3 + 2], X[:], ALU.mult, ALU.add),
                                 reads=[b_U, b_Uh, b_X, b_cw], writes=[b_X])
                            c.op("dve", lambda e, X=X, U=U, ch=ch: e.scalar_tensor_tensor(X[:], U[:, 0:512], cw[:, ch * 3:ch * 3 + 1], X[:], ALU.mult, ALU.add),
                                 reads=[b_U, b_Uh, b_X, b_cw], writes=[b_X])
                            xr[role] = (X, b_X)
                        return xr

                    def ffn_back(cp, xr):
                        Xa, b_Xa = xr["a"]
                        Xg, b_Xg = xr["g"]
                        c.op("act", lambda e: e.activation(Xg[:], Xg[:], AF.Silu), reads=[b_Xg], writes=[b_Xg])
                        c.op("pool", lambda e: e.tensor_tensor(actT[:, cp, :], Xa[:], Xg[:], ALU.mult),
                             reads=[b_Xa, b_Xg], writes=[b_actT[cp]])
                    cur = ffn_front(0)
                    for cp in range(NFC):
                        if blk == 0:
                            if cp % 2 == 0:
                                load_win_group()
                                load_win_group()
                            load_wout_chunk()
                        if cp == 17 and blk + 1 < NB:
                            norm_block.stage1(h_dram, hd_bufs, blk + 1)
                        nx = ffn_front(cp + 1) if cp + 1 < NFC else None
                        ffn_back(cp, cur)
                        cur = nx
                    if blk == 0:
                        while wstate["g"] < 22:
                            load_win_group()
                        while wstate["o"] < NFC:
                            load_wout_chunk()
                    NE = 16 if blk > 0 else 8
                    groups = [(j, half) for j in range(4) for half in range(2)]
                    banks = list(wo_rot.items) + list(mm.items)
                    nearly = min(len(banks), len(groups))

                    def mm_part(pw, j, half, f0, f1):
                        def fn(e):
                            ins = None
                            for fc in range(f0, f1):
                                ins = e.matmul(pw[:], actT[:, fc, j * 128:(j + 1) * 128], wout[:, fc, half * 512:(half + 1) * 512],
                                               start=(fc == 0), stop=(fc == NFC - 1))
                            return ins
                        return fn
                    for gi in range(nearly):
                        j, half = groups[gi]
                        pw, b_pw = banks[gi]
                        c.op("pe", mm_part(pw, j, half, 0, NE), reads=b_actT[0:NE] + b_woutc[0:NE], writes=[b_pw])
                    if blk + 1 < NB:
                        norm_block.stage2(hnT, b_hnT, ptr_rot)
                    ht_cur = None
                    for gi, (j, half) in enumerate(groups):
                        t = blk * 4 + j
                        if half == 0:
                            ht, b_ht = hts.next()
                            c.dma("sp", ht[:], h_dram[t * 128:(t + 1) * 128, :], b_ht, reads=[hd_bufs[t]], writes=[b_ht])
                            ht_cur = (ht, b_ht)
                        ht, b_ht = ht_cur
                        if gi < nearly:
                            pw, b_pw = banks[gi]
                            c.op("pe", mm_part(pw, j, half, NE, NFC), reads=b_actT[NE:] + b_woutc[NE:], writes=[b_pw])
                        else:
                            pw, b_pw = banks[gi - nearly]
                            c.op("pe", mm_part(pw, j, half, 0, NFC), reads=b_actT + b_woutc, writes=[b_pw])
                        c.op("dve", lambda e, ht=ht, pw=pw, half=half: e.tensor_tensor(ht[:, half * 512:(half + 1) * 512], ht[:, half * 512:(half + 1) * 512], pw[:], ALU.add),
                             reads=[b_pw, b_ht], writes=[b_ht])
                        if half == 1:
                            emit_ssq(ssq_out, ht, b_ht, t, junk, b_junk)
                            c.dma("sp", h_dram[t * 128:(t + 1) * 128, :], ht[:], b_ht, reads=[b_ht], writes=[hd_bufs[t]])
                c.barrier()

        def make_rope(st):
            cosr = Rot([(sb(st, "cos%d" % i, [128, 64], F32), c.buf("cos")) for i in range(2)])
            sinr = Rot([(sb(st, "sin%d" % i, [128, 64], F32), c.buf("sin")) for i in range(2)])
            t1r = Rot([(sb(st, "t1_%d" % i, [128, D], F32), c.buf("t1")) for i in range(2)])
            t2r = Rot([(sb(st, "t2_%d" % i, [128, D], F32), c.buf("t2")) for i in range(2)])

            def rope(xs, b_xs, t, outb, b_outb):
                cs, b_cs = cosr.next()
                sn, b_sn = sinr.next()
                c.dma("sp", cs[:], rope_in[:, t * 64:(t + 1) * 64], b_cs, writes=[b_cs])
                c.dma("sp", sn[:], rope_in[:, NT * 64 + t * 64:NT * 64 + (t + 1) * 64], b_sn, writes=[b_sn])
                t1, b_t1 = t1r.next()
                t2, b_t2 = t2r.next()
                x3 = xs[:].rearrange("p (g d) -> p g d", d=64)
                t13 = t1[:].rearrange("p (g d) -> p g d", d=64)
                t23 = t2[:].rearrange("p (g d) -> p g d", d=64)
                c.op("dve", lambda e: e.tensor_tensor(t13, x3, cs[:].unsqueeze(1).to_broadcast([128, 16, 64]), ALU.mult),
                     reads=[b_xs, b_cs], writes=[b_t1])
                c.op("pool", lambda e: e.tensor_tensor(t23[:, :, 0:32], x3[:, :, 32:64], sn[:, 0:32].unsqueeze(1).to_broadcast([128, 16, 32]), ALU.mult),
                     reads=[b_xs, b_sn], writes=[b_t2])
                c.op("pool", lambda e: e.tensor_tensor(t23[:, :, 32:64], x3[:, :, 0:32], sn[:, 32:64].unsqueeze(1).to_broadcast([128, 16, 32]), ALU.mult),
                     reads=[b_xs, b_sn], writes=[b_t2])
                c.op("dve", lambda e: e.tensor_tensor(outb[:], t1[:], t2[:], ALU.add), reads=[b_t1, b_t2], writes=[b_outb])
            return rope

        def phase_QV(VE, b_VE):
            with ExitStack() as st:
                wq = sb(st, "Q_wq", [128, NKC, D], BF16)
                b_wq = c.buf("Q_wq")
                wv = sb(st, "Q_wv", [128, NKC, D], BF16)
                b_wv = c.buf("Q_wv")
                with ExitStack() as st2:
                    stg = mk_stage(st2)
                    load_w(st2, wq, b_wq, wq_in, NKC, D, stg, scale_col=24, cmul=64 ** -0.5)
                    load_w(st2, wv, b_wv, wkv_in, NKC, D, stg, scale_col=16, src_col0=1024)
                    c.barrier()
                c.op("pool", lambda e: e.memset(VE[:], 1.0), writes=[b_VE])
                norm_block, hts = make_norm(st, pre="B")
                rope = make_rope(st)
                hnT = sb(st, "Q_hnT", [128, NKC, 512], BF16)
                b_hnT = c.buf("Q_hnT")
                xsr = Rot([(sb(st, "Q_xs%d" % i, [128, D], F32), c.buf("xs")) for i in range(2)])
                qbr = Rot([(sb(st, "Q_qb%d" % i, [128, D], BF16), c.buf("qb")) for i in range(2)])
                qst = sb(st, "Q_qst", [128, 8, 2, 512], BF16)
                b_qst = c.buf("qst")
                c.op("pool", lambda e: e.memset(qst[:], 0.0), writes=[b_qst])
                mm = Rot([(ps(st, "Q_mm%d" % i, [128, 512], F32), c.buf("mm", True)) for i in range(4)])
                ptr_rot = Rot([(ps(st, "Q_ptr%d" % i, [128, NKC, 128], BF16), c.buf("ptr", True)) for i in range(2)])
                pend_q = []
                norm_block(h_dram, hd_bufs, 0, hnT, b_hnT, ptr_rot)
                for blk in range(NB):
                    for j in range(4):
                        t = blk * 4 + j
                        xs, b_xs = xsr.next()
                        for half in range(2):
                            pq, b_pq = mm.next()

                            def mm_q(e, pq=pq, j=j, half=half):
                                ins = None
                                for kc in range(NKC):
                                    ins = e.matmul(pq[:], hnT[:, kc, j * 128:(j + 1) * 128], wq[:, kc, half * 512:(half + 1) * 512],
                                                   start=(kc == 0), stop=(kc == NKC - 1))
                                return ins
                            c.op("pe", mm_q, reads=[b_wq, b_hnT], writes=[b_pq])
                            c.op("act", lambda e, xs=xs, pq=pq, half=half: e.copy(xs[:, half * 512:(half + 1) * 512], pq[:]), reads=[b_pq], writes=[b_xs])
                        for half in range(2):
                            pv, b_pv = mm.next()

                            def mm_v(e, pv=pv, j=j, half=half):
                                ins = None
                                for kc in range(NKC):
                                    ins = e.matmul(pv[:], hnT[:, kc, j * 128:(j + 1) * 128], wv[:, kc, half * 512:(half + 1) * 512],
                                                   start=(kc == 0), stop=(kc == NKC - 1))
                                return ins
                            c.op("pe", mm_v, reads=[b_wv, b_hnT], writes=[b_pv])
                            c.op("dve", lambda e, pv=pv, t=t, half=half: e.tensor_copy(VE[:, t, half * 4:(half + 1) * 4, 0:128], pv[:].rearrange("p (h e) -> p h e", e=128)),
                                 reads=[b_pv], writes=[b_VE])
                        qb_, b_qb = qbr.next()
                        rope(xs, b_xs, t, qb_, b_qb)

                        def finish_q(qb_=qb_, b_qb=b_qb, j=j):
                            ptr, b_ptr = ptr_rot.next()

                            def tr_q(e):
                                ins = None
                                for hh in range(8):
                                    ins = e.transpose(ptr[:, hh, :], qb_[:, hh * 128:(hh + 1) * 128], identb[:])
                                return ins
                            c.op("pe", tr_q, reads=[b_qb, b_identb], writes=[b_ptr])
                            c.op("act", lambda e: e.copy(qst[0:64, :, 0, j * 128:(j + 1) * 128], ptr[0:64, :, :]), reads=[b_ptr], writes=[b_qst])
                            c.op("act", lambda e: e.copy(qst[64:128, :, 1, j * 128:(j + 1) * 128], ptr[64:128, :, :]), reads=[b_ptr], writes=[b_qst])
                        if pend_q:
                            pend_q.pop()()
                        pend_q.append(finish_q)
                    if blk + 1 < NB:
                        norm_block(h_dram, hd_bufs, blk + 1, hnT, b_hnT, ptr_rot)
                    pend_q.pop()()
                    c.dma("sp", qt_dram[blk * 1024:(blk + 1) * 1024, :].rearrange("(h p) c -> p h c", p=128), qst[:].rearrange("p h s t -> p h (s t)"),
                          b_qst, reads=[b_qst], writes=qt_bufs[blk * 8:(blk + 1) * 8])
                c.barrier()

        def phase_K(KT, b_KT):
            with ExitStack() as st:
                wk = sb(st, "K_wk", [128, NKC, D], BF16)
                b_wk = c.buf("K_wk")
                with ExitStack() as st2:
                    stg = mk_stage(st2)
                    load_w(st2, wk, b_wk, wkv_in, NKC, D, stg, scale_col=16, src_col0=0)
                    c.barrier()
                norm_block, hts = make_norm(st, pre="B")
                rope = make_rope(st)
                hnT = sb(st, "K_hnT", [128, NKC, 512], BF16)
                b_hnT = c.buf("K_hnT")
                xsr = Rot([(sb(st, "K_xs%d" % i, [128, D], F32), c.buf("xs")) for i in range(2)])
                kbr = Rot([(sb(st, "K_kb%d" % i, [128, D], BF16), c.buf("kb")) for i in range(2)])
                mm = Rot([(ps(st, "K_mm%d" % i, [128, 512], F32), c.buf("mm", True)) for i in range(4)])
                ptr_rot = Rot([(ps(st, "K_ptr%d" % i, [128, NKC, 128], BF16), c.buf("ptr", True)) for i in range(2)])
                pend_k = []
                norm_block(h_dram, hd_bufs, 0, hnT, b_hnT, ptr_rot)
                for blk in range(NB):
                    for j in range(4):
                        t = blk * 4 + j
                        xs, b_xs = xsr.next()
                        for half in range(2):
                            pk_, b_pk = mm.next()

                            def mm_k(e, pk_=pk_, j=j, half=half):
                                ins = None
                                for kc in range(NKC):
                                    ins = e.matmul(pk_[:], hnT[:, kc, j * 128:(j + 1) * 128], wk[:, kc, half * 512:(half + 1) * 512],
                                                   start=(kc == 0), stop=(kc == NKC - 1))
                                return ins
                            c.op("pe", mm_k, reads=[b_wk, b_hnT], writes=[b_pk])
                            c.op("act", lambda e, xs=xs, pk_=pk_, half=half: e.copy(xs[:, half * 512:(half + 1) * 512], pk_[:]), reads=[b_pk], writes=[b_xs])
                        kb_, b_kb = kbr.next()
                        rope(xs, b_xs, t, kb_, b_kb)

                        def finish_k(kb_=kb_, b_kb=b_kb, t=t):
                            ptr, b_ptr = ptr_rot.next()

                            def tr_k(e):
                                ins = None
                                for hh in range(8):
                                    ins = e.transpose(ptr[:, hh, :], kb_[:, hh * 128:(hh + 1) * 128], identb[:])
                                return ins
                            c.op("pe", tr_k, reads=[b_kb, b_identb], writes=[b_ptr])
                            c.op("act", lambda e: e.copy(KT[:, :, t * 128:(t + 1) * 128], ptr[:]), reads=[b_ptr], writes=b_KT)
                        if pend_k:
                            pend_k.pop()()
                        pend_k.append(finish_k)
                    if blk + 1 < NB:
                        norm_block(h_dram, hd_bufs, blk + 1, hnT, b_hnT, ptr_rot)
                    pend_k.pop()()
                c.barrier()

        def phase_D(KT, b_KT, VE, b_VE):
            with ExitStack() as st:
                wo = sb(st, "D_wo", [128, NKC, D], BF16)
                b_wo = c.buf("D_wo")
                lamt = sb(st, "D_lamt", [128, 256], F32)
                b_lamt = c.buf("lamt")
                lsc = sb(st, "D_lsc", [128, 8], F32)
                b_lsc = c.buf("lsc")
                subw = sb(st, "D_subw", [128, 128], F32)
                b_subw = c.buf("subw")
                wo_state = {"kc": 0}

                def load_wo_chunk():
                    kc = wo_state["kc"]
                    if kc >= NKC:
                        return
                    wo_state["kc"] += 1
                    ht, b_ht = hts.next()
                    c.dma("sp", ht[:], dwo_in[kc * 128:(kc + 1) * 128, :], b_ht, writes=[b_ht])
                    c.op("dve", lambda e: e.tensor_copy(wo[:, kc, :], ht[:]), reads=[b_ht], writes=[b_wo])
                c.dma("sp", lamt[:], lam_in.partition_broadcast(128), b_lamt, writes=[b_lamt])
                c.dma("sp", subw[:], subw_in.partition_broadcast(128), b_subw, writes=[b_subw])
                c.op("dve", lambda e: e.tensor_scalar(subw[:], subw[:], 1.0 - LAM_INIT, None, ALU.mult), reads=[b_subw], writes=[b_subw])
                l4 = lamt[:].rearrange("p (a b) -> p a b", b=64)
                c.op("dve", lambda e: e.tensor_tensor(l4[:, 0:1, :], l4[:, 0:1, :], l4[:, 1:2, :], ALU.mult), reads=[b_lamt], writes=[b_lamt])
                c.op("dve", lambda e: e.tensor_tensor(l4[:, 2:3, :], l4[:, 2:3, :], l4[:, 3:4, :], ALU.mult), reads=[b_lamt], writes=[b_lamt])
                c.op("dve", lambda e: e.tensor_reduce(lsc[:, 0:1], lamt[:, 0:64], AX.X, ALU.add), reads=[b_lamt], writes=[b_lsc])
                c.op("dve", lambda e: e.tensor_reduce(lsc[:, 1:2], lamt[:, 128:192], AX.X, ALU.add), reads=[b_lamt], writes=[b_lsc])
                c.op("act", lambda e: e.activation(lsc[:, 2:4], lsc[:, 0:2], AF.Exp), reads=[b_lsc], writes=[b_lsc])
                c.op("dve", lambda e: e.tensor_tensor(lsc[:, 4:5], lsc[:, 3:4], lsc[:, 2:3], ALU.subtract), reads=[b_lsc], writes=[b_lsc])
                c.op("dve", lambda e: e.tensor_scalar(lsc[:, 5:6], lsc[:, 4:5], -LAM_INIT, None, ALU.add), reads=[b_lsc], writes=[b_lsc])
                nlam = lsc[:, 5:6]
                hts = Rot([(sb(st, "D_ht%d" % i, [128, D], F32), c.buf("ht")) for i in range(3)])
                qtr = Rot([(sb(st, "D_qt%d" % i, [128, 2, 512], BF16), c.buf("qt")) for i in range(3)])
                ptr_ = Rot([(sb(st, "D_pt%d" % i, [128, 512], BF16), c.buf("pt")) for i in range(4)])
                eOr = Rot([(sb(st, "D_eO%d" % i, [128, 2, 512], F32), c.buf("eO")) for i in range(2)])
                eSr = Rot([(sb(st, "D_eS%d" % i, [128, 2, 512], F32), c.buf("eS")) for i in range(2)])
                osqr = Rot([(sb(st, "D_osq%d" % i, [128, 512], BF16), c.buf("osq")) for i in range(2)])
                yT = sb(st, "D_yT", [128, 8, 512], BF16)
                b_yT = [c.buf("yT") for _ in range(8)]
                onesb = sb(st, "D_ones", [128, 128], BF16)
                b_onesb = c.buf("ones")
                c.op("pool", lambda e: e.memset(onesb[:], 1.0), writes=[b_onesb])
                subwc = sb(st, "D_subwc", [128, 1], F32)
                b_subwc = c.buf("subwc")
                c.dma("sp", subwc[:], subw_in.rearrange("(p o) -> p o", o=1), b_subwc, writes=[b_subwc])
                c.op("dve", lambda e: e.tensor_scalar(subwc[:], subwc[:], 1.0 - LAM_INIT, None, ALU.mult), reads=[b_subwc], writes=[b_subwc])
                pending_epi = []
                head_res = {}
                pending_wo = []
                pending_red = []
                junk = sb(st, "D_junk", [128, D], BF16)
                b_junk = c.buf("junk")
                stp = Rot([(ps(st, "D_st%d" % i, [128, 512], F32), c.buf("st", True)) for i in range(3)])
                accO = [(ps(st, "D_accO%d" % i, [128, 512], F32), c.buf("accO", True)) for i in range(2)]
                accS = [(ps(st, "D_accS%d" % i, [128, 512], F32), c.buf("accS", True)) for i in range(2)]
                pwr = Rot([(ps(st, "D_pw%d" % i, [128, 512], F32), c.buf("pw", True)) for i in range(1)])
                for qb in range(NB):
                    nkt = 4 * qb + 4
                    items = [(h, comp, kt) for h in range(8) for comp in range(2) for kt in range(nkt)]
                    def head_ctx(h, qb=qb):
                        if (qb, h) not in head_res:
                            qt, b_qt = qtr.next()
                            c.dma("sp", qt[:].rearrange("p s t -> p (s t)"), qt_dram[(qb * 8 + h) * 128:(qb * 8 + h + 1) * 128, :], b_qt,
                                  reads=[qt_bufs[qb * 8 + h]], writes=[b_qt])
                            eO, b_eO = eOr.next()
                            eS, b_eS = eSr.next()
                            head_res[(qb, h)] = (qt, b_qt, eO, b_eO, eS, b_eS)
                        return head_res[(qb, h)]

                    def emit_qk(idx):
                        h, comp, kt = items[idx]
                        qt, b_qt = head_ctx(h)[0:2]
                        o = kt * 128 - qb * 512
                        lo = max(o, 0)
                        pst, b_pst = stp.next()
                        if o >= 0:
                            def qk(e):
                                e.matmul(pst[:, lo:512], KT[:, h, kt * 128:(kt + 1) * 128], qt[:, comp, lo:512], start=True, stop=False)
                                return e.matmul(pst[:, o:o + 128], identb[:], negmask[:], start=False, stop=True)
                            c.op("pe", qk, reads=[b_KT[h], b_qt, b_identb, b_negmask], writes=[b_pst])
                        else:
                            c.op("pe", lambda e: e.matmul(pst[:, lo:512], KT[:, h, kt * 128:(kt + 1) * 128], qt[:, comp, lo:512], start=True, stop=True),
                                 reads=[b_KT[h], b_qt], writes=[b_pst])
                        return pst, b_pst
                    LOOK = 2
                    n_head = 2 * nkt
                    epi_slots = (1, nkt)
                    inflight = [emit_qk(i) for i in range(min(LOOK, len(items)))]
                    for idx, (h, comp, kt) in enumerate(items):
                        qt, b_qt, eO, b_eO, eS, b_eS = head_ctx(h)
                        aO, b_aO = accO[comp]
                        aS, b_aS = accS[comp]
                        o = kt * 128 - qb * 512
                        lo = max(o, 0)
                        pst, b_pst = inflight.pop(0)
                        pt, b_pt = ptr_.next()
                        c.op("act", lambda e, pt=pt, pst=pst, lo=lo: e.activation(pt[:, lo:512], pst[:, lo:512], AF.Exp), reads=[b_pst], writes=[b_pt])
                        if idx + LOOK < len(items):
                            inflight.append(emit_qk(idx + LOOK))
                        rel = comp * nkt + kt
                        if rel == 2:
                            if h < 7:
                                head_ctx(h + 1)
                            elif qb + 1 < NB:
                                head_ctx(0, qb + 1)
                        if rel in epi_slots and pending_epi:
                            pending_epi.pop(0)()
                        if qb == 0 and idx % 4 == 2:
                            load_wo_chunk()
                        if idx >= nkt + 1 and pending_wo:
                            pending_wo.pop(0)()
                        first = (kt == 0)
                        last = (kt == nkt - 1)
                        c.op("pe", lambda e, aO=aO, pt=pt, kt=kt, h=h, lo=lo, first=first, last=last: e.matmul(aO[:, lo:512], VE[:, kt, h, 0:128], pt[:, lo:512], start=first, stop=last),
                             reads=[b_pt, b_VE], writes=[b_aO])
                        c.op("pe", lambda e, aS=aS, pt=pt, lo=lo, first=first, last=last: e.matmul(aS[:, lo:512], onesb[:], pt[:, lo:512], start=first, stop=last),
                             reads=[b_pt, b_onesb], writes=[b_aS])
                        if last:
                            c.op("dve", lambda e, aS=aS, comp=comp, eS=eS: e.reciprocal(eS[:, comp, :], aS[:]), reads=[b_aS], writes=[b_eS])
                            if comp == 0:
                                c.op("dve", lambda e, aO=aO, eO=eO, eS=eS: e.tensor_tensor(eO[:, 0, :], aO[:], eS[:, 0, :], ALU.mult),
                                     reads=[b_aO, b_eS], writes=[b_eO])
                            else:
                                c.op("dve", lambda e, aO=aO, eO=eO, eS=eS: e.scalar_tensor_tensor(eO[:, 1, :], aO[:], nlam, eS[:, 1, :], ALU.mult, ALU.mult),
                                     reads=[b_aO, b_eS, b_lsc], writes=[b_eO])
                        if last and comp == 1:
                            def epilogue(eO=eO, b_eO=b_eO, eS=eS, b_eS=b_eS, h=h):
                                osq, b_osq = osqr.next()
                                pss, b_pss = pwr.next()

                                def p1():
                                    c.op("dve", lambda e: e.tensor_tensor(eO[:, 0, :], eO[:, 0, :], eO[:, 1, :], ALU.add), reads=[b_eO], writes=[b_eO])
                                    c.op("dve", lambda e: e.tensor_tensor(osq[:], eO[:, 0, :], eO[:, 0, :], ALU.mult), reads=[b_eO], writes=[b_osq])

                                def p2():
                                    c.op("pe", lambda e: e.matmul(pss[:], onesb[:], osq[:], start=True, stop=True), reads=[b_osq, b_onesb], writes=[b_pss])

                                def p3():
                                    c.op("act", lambda e: e.activation(eS[:, 0, :], pss[:], AF.Ln, bias=EPS, scale=1.0 / 128), reads=[b_pss], writes=[b_eS])
                                    c.op("act", lambda e: e.activation(eS[:, 0, :], eS[:, 0, :], AF.Exp, scale=-0.5), reads=[b_eS], writes=[b_eS])
                                    c.op("dve", lambda e: e.scalar_tensor_tensor(yT[:, h, :], eO[:, 0, :], subwc[:, 0:1], eS[:, 0, :], ALU.mult, ALU.mult),
                                         reads=[b_eO, b_eS, b_subwc], writes=[b_yT[h]])
                                def p23():
                                    p2()
                                    p3()
                                return [p1, p23]
                            pending_epi.extend(epilogue())
                    wctx = {}

                    def stepL(s_, qb=qb):
                        def f():
                            t = qb * 4 + s_
                            ht, b_ht = hts.next()
                            c.dma("sp", ht[:], h_dram[t * 128:(t + 1) * 128, :], b_ht, reads=[hd_bufs[t]], writes=[b_ht])
                            wctx[(qb, s_)] = (ht, b_ht)
                        return f

                    def stepM(s_, half, qb=qb):
                        def f():
                            ht, b_ht = wctx[(qb, s_)]
                            pw, b_pw = pwr.next()

                            def mm_wo(e):
                                ins = None
                                for kc in range(NKC):
                                    ins = e.matmul(pw[:], yT[:, kc, s_ * 128:(s_ + 1) * 128], wo[:, kc, half * 512:(half + 1) * 512], start=(kc == 0), stop=(kc == NKC - 1))
                                return ins
                            c.op("pe", mm_wo, reads=b_yT + [b_wo], writes=[b_pw])
                            c.op("dve", lambda e: e.tensor_tensor(ht[:, half * 512:(half + 1) * 512], ht[:, half * 512:(half + 1) * 512], pw[:], ALU.add),
                                 reads=[b_pw, b_ht], writes=[b_ht])
                        return f

                    def stepS(s_, qb=qb):
                        def f():
                            t = qb * 4 + s_
                            ht, b_ht = wctx.pop((qb, s_))
                            emit_ssq("D", ht, b_ht, t, junk, b_junk)
                            c.dma("sp", h_dram[t * 128:(t + 1) * 128, :], ht[:], b_ht, reads=[b_ht], writes=[hd_bufs[t]])
                        return f
                    pending_wo.extend([stepL(0), stepL(1), stepL(2), stepM(0, 0), stepM(0, 1), stepM(1, 0), stepM(1, 1), stepS(0), stepL(3),
                                       stepM(2, 0), stepM(2, 1), stepS(1), stepM(3, 0), stepM(3, 1), stepS(2), stepS(3)])
                while pending_epi:
                    pending_epi.pop(0)()
                while pending_wo:
                    pending_wo.pop(0)()
                c.barrier()

        def phase_G():
            with ExitStack() as st:
                fnw = sb(st, "G_fnw", [128, D], F32)
                b_fnw = c.buf("fnw")
                c.dma("sp", fnw[:], fnw_in.partition_broadcast(128), b_fnw, writes=[b_fnw])
                lnv = sb(st, "G_lnv", [128, NT], F32)
                b_lnv = c.buf("lnv")
                rstd = sb(st, "G_rstd", [128, NT], F32)
                b_rstd = c.buf("rstd")
                sq, b_sq = ssq_glob["E"]
                c.op("act", lambda e: e.activation(lnv[:], sq[:], AF.Ln, bias=EPS, scale=1.0 / D), reads=[b_sq], writes=[b_lnv])
                c.op("act", lambda e: e.activation(rstd[:], lnv[:], AF.Exp, scale=-0.5), reads=[b_lnv], writes=[b_rstd])
                hts = Rot([(sb(st, "G_ht%d" % i, [128, D], F32), c.buf("ht")) for i in range(4)])
                for t in range(NT):
                    ht, b_ht = hts.next()
                    c.dma("sp", ht[:], h_dram[t * 128:(t + 1) * 128, :], b_ht, reads=[hd_bufs[t]], writes=[b_ht])
                    en = ("dve", "pool")[0]
                    c.op(en, lambda e, ht=ht, t=t: e.scalar_tensor_tensor(ht[:], ht[:], rstd[:, t:t + 1], fnw[:], ALU.mult, ALU.mult),
                         reads=[b_ht, b_rstd, b_fnw], writes=[b_ht])
                    c.dma("act", out_dram[t * 128:(t + 1) * 128, :], ht[:], b_ht, reads=[b_ht])
                c.barrier()

        if "A" in phases:
            phase_A()
        if "B" in phases:
            phase_F(0, "A", "B")
        if "Q" in phases or "K" in phases or "D" in phases:
            with ExitStack() as kvst:
                VE = sb(kvst, "VE", [128, NT, 8, 129], BF16)
                b_VE = c.buf("VE")
                if "Q" in phases:
                    phase_QV(VE, b_VE)
                KT = sb(kvst, "KT", [128, 8, S], BF16)
                b_KT = [c.buf("KT") for _ in range(8)]
                if "K" in phases:
                    phase_K(KT, b_KT)
                if "D" in phases:
                    phase_D(KT, b_KT, VE, b_VE)
                c.barrier()
        if "E" in phases:
            phase_F(1, "D", "E")
            phase_G()
        c.barrier()
        print("build: nins", c.nins, "nwait", c.nwait, "nsem", len(c.owners) + 5)
        if limit is not None:
            print("LAST:", str(c.last_desc)[:600])
    return nc


def make_consts(S):
    NT = S // 128
    ident = np.eye(128, dtype=np.float32)
    j = np.arange(128)[:, None]
    cidx = np.arange(128)[None, :]
    glamask = ((j <= cidx) & ((j // 64) == (cidx // 64))).astype(np.float32)
    trimask = (j <= cidx).astype(np.float32)
    scan = np.ones((128, 512), np.float32)
    scan[:, ::64] = 0.0
    cst = np.concatenate([ident, glamask, trimask, scan], axis=1)
    pos = np.arange(S, dtype=np.float32)
    inv = (10000.0 ** (-np.arange(0, 64, 2, dtype=np.float32) / 64)).astype(np.float32)
    f = pos[:, None] * inv[None, :]
    emb = np.concatenate([f, f], axis=-1)
    cos = np.cos(emb).astype(np.float32)
    sin = np.sin(emb).astype(np.float32)
    ssin = np.concatenate([-sin[:, :32], sin[:, 32:]], axis=1)
    cos_t = cos.reshape(NT, 128, 64).transpose(1, 0, 2).reshape(128, NT * 64)
    ssin_t = ssin.reshape(NT, 128, 64).transpose(1, 0, 2).reshape(128, NT * 64)
    rope = np.ascontiguousarray(np.concatenate([cos_t, ssin_t], axis=1))
    return np.ascontiguousarray(cst), rope


def pk(v):
    return np.ascontiguousarray(np.asarray(v, np.float32).reshape(8, 128).T)


def shared_inputs(inp, S):
    f = lambda a: np.ascontiguousarray(np.asarray(a, dtype=np.float32))
    cst, rope = make_consts(S)
    nw = np.concatenate([pk(inp["attn_norm_w"][0]), pk(inp["ffn_norm_w"][0]), pk(inp["kv_norm_w"]),
                         pk(inp["attn_norm_w"][1]), pk(inp["ffn_norm_w"][1])], axis=1)
    cw = np.asarray(inp["ffn_conv_w"], np.float32)
    cw = cw.reshape(2, 3, 44, 128).transpose(0, 3, 2, 1).reshape(2, 128, 44 * 3)
    cb = np.asarray(inp["ffn_conv_b"], np.float32).reshape(2, 44, 128).transpose(0, 2, 1)
    return {
        "nw": np.ascontiguousarray(nw), "fnw": f(inp["final_norm_w"]),
        "wqkvg": f(inp["gla_w_qkvg"][0]), "wgk1": f(inp["gla_w_gk1"][0]), "wgk2": f(inp["gla_w_gk2"][0]),
        "bgk": np.ascontiguousarray(np.asarray(inp["gla_b_gk"][0], np.float32).reshape(4, 128).T),
        "onw": f(inp["gla_onorm_w"][0]), "gwo": f(inp["gla_w_o"][0]),
        "wkv": f(inp["w_kv"]), "wq": f(inp["diff_w_q"][0]), "lam": f(np.asarray(inp["diff_lambda"][0]).reshape(256)),
        "subw": f(inp["diff_subln_w"][0]), "dwo": f(inp["diff_w_o"][0]),
        "win": f(inp["ffn_w_in"]), "cw": np.ascontiguousarray(cw), "cb": np.ascontiguousarray(cb),
        "wout": f(inp["ffn_w_out"]), "cst": cst, "rope": rope,
    }


_NC_CACHE = {}


def kernel(**inputs):
    x = np.asarray(inputs["x"], dtype=np.float32)
    B, S, _ = x.shape
    if S not in _NC_CACHE:
        _NC_CACHE[S] = build(S)
    nc = _NC_CACHE[S]
    sh = shared_inputs(inputs, S)
    in_maps = []
    for b in range(B):
        m = dict(sh)
        m["x"] = np.ascontiguousarray(x[b])
        in_maps.append(m)
    res = run_bass_kernel_spmd(nc, in_maps, core_ids=list(range(B)))
    return np.stack([np.asarray(r["out"]) for r in res.results], axis=0).astype(np.float32)
```
